# Optimizing a Trainium2 kernel written in Bass

```python
import math
import jax, jax.numpy as jnp
from jax import lax
import numpy as np

D_MODEL = 1024
BATCH = 32
SEQ = 2048
DEPTH = 1

N_HEADS = 8
HEAD_DIM = 64
V_DIM = 2 * HEAD_DIM
ATTN_QK = N_HEADS * 2 * HEAD_DIM
ATTN_V = N_HEADS * V_DIM
Q_BLOCK = 128
POOL_WINDOWS = (2, 4, 8, 16)
N_POOL_GROUPS = len(POOL_WINDOWS)
POOL_GROUP_IN = 128
POOL_IN = N_POOL_GROUPS * POOL_GROUP_IN
POOL_GROUP_OUT = D_MODEL // N_POOL_GROUPS
N_BRANCHES = 2
IN_COLS = 2 * ATTN_QK + ATTN_V + POOL_IN + N_BRANCHES * D_MODEL
D_FF = -(-8 * D_MODEL // (3 * 256)) * 256
ALPHA = (2 * DEPTH) ** 0.25
BETA = (8 * DEPTH) ** -0.25
LN_EPS = 1e-5
RMS_EPS = 1e-5
N_MOD = 6

kernel_name = "hybrid_diffattn_multipool_deepnorm_block"


def alibi_slopes(n_heads):
    return np.array([2.0 ** (-8.0 * (h + 1) / n_heads) for h in range(n_heads)], dtype=np.float32)


def layer_norm(x, g=None, b=None):
    xf = x.astype(jnp.float32)
    mu = jnp.mean(xf, axis=-1, keepdims=True)
    var = jnp.mean(jnp.square(xf - mu), axis=-1, keepdims=True)
    y = (xf - mu) * lax.rsqrt(var + LN_EPS)
    if g is not None:
        y = y * g.astype(jnp.float32) + b.astype(jnp.float32)
    return y.astype(x.dtype)


def modulate(h, shift, scale):
    return h * (1.0 + scale[:, None, :]) + shift[:, None, :]


def diff_attention(q, k, v, lam, slopes, sub_g, lambda_init):
    B, S = q.shape[0], q.shape[1]
    nb = S // Q_BLOCK
    scale = HEAD_DIM ** -0.5
    kpos = jnp.arange(S)
    qb = q.reshape(B, nb, Q_BLOCK, N_HEADS, 2, HEAD_DIM).transpose(1, 0, 2, 3, 4, 5)

    def block(args):
        qblk, i = args
        qpos = i * Q_BLOCK + jnp.arange(Q_BLOCK)
        dist = jnp.abs(qpos[:, None] - kpos[None, :]).astype(jnp.float32)
        bias = -slopes[:, None, None] * dist[None]
        s = jnp.einsum('bqhmd,bkhmd->bhmqk', qblk, k,
                       preferred_element_type=jnp.float32) * scale + bias[None, :, None]
        p = jax.nn.softmax(s, axis=-1)
        a = p[:, :, 0] - lam * p[:, :, 1]
        return jnp.einsum('bhqk,bkhe->bqhe', a, v.astype(jnp.float32))

    o = lax.map(block, (qb, jnp.arange(nb)))
    o = o.transpose(1, 0, 2, 3, 4).reshape(B, S, N_HEADS, V_DIM)
    o = o * lax.rsqrt(jnp.mean(jnp.square(o), axis=-1, keepdims=True) + RMS_EPS)
    o = o * sub_g.astype(jnp.float32) * (1.0 - lambda_init)
    return o.reshape(B, S, ATTN_V).astype(q.dtype)


def multiscale_pool(u, w_pool, pool_scale):
    B, S = u.shape[0], u.shape[1]
    ug = u.astype(jnp.float32).reshape(B, S, N_POOL_GROUPS, POOL_GROUP_IN)
    csum = jnp.concatenate(
        [jnp.zeros((B, 1, N_POOL_GROUPS, POOL_GROUP_IN), jnp.float32), jnp.cumsum(ug, axis=1)], axis=1)
    t = jnp.arange(S)
    half = jnp.array([w // 2 for w in POOL_WINDOWS])
    lo = jnp.clip(t[:, None] - half[None, :], 0, S)
    hi = jnp.clip(t[:, None] + half[None, :], 0, S)
    g_idx = jnp.arange(N_POOL_GROUPS)[None, :]
    win_sum = csum[:, hi, g_idx] - csum[:, lo, g_idx]
    cnt = (hi - lo).astype(jnp.float32)[None, :, :, None]
    pooled = win_sum / cnt - ug
    y = jnp.einsum('bsgc,gce->bsge', pooled, w_pool.astype(jnp.float32))
    y = y.reshape(B, S, D_MODEL) * pool_scale.astype(jnp.float32)
    return y.astype(u.dtype)


def setup_inputs(seed: int = 0) -> dict:
    key = jax.random.key(seed)
    ks = jax.random.split(key, 20)
    n = jax.random.normal
    f32 = jnp.float32
    L, D = DEPTH, D_MODEL
    return {
        "x": n(ks[0], (BATCH, SEQ, D), f32),
        "c": n(ks[1], (BATCH, D), f32),
        "w_ada": n(ks[2], (L, D, N_MOD * D), f32) * (0.5 * D ** -0.5),
        "b_ada": n(ks[3], (L, N_MOD * D), f32) * 0.01,
        "w_in": n(ks[4], (L, D, IN_COLS), f32) * D ** -0.5,
        "lambda_q1": n(ks[5], (L, HEAD_DIM), f32) * 0.1,
        "lambda_k1": n(ks[6], (L, HEAD_DIM), f32) * 0.1,
        "lambda_q2": n(ks[7], (L, HEAD_DIM), f32) * 0.1,
        "lambda_k2": n(ks[8], (L, HEAD_DIM), f32) * 0.1,
        "sub_g": 1.0 + 0.02 * n(ks[9], (L, V_DIM), f32),
        "w_pool": n(ks[10], (L, N_POOL_GROUPS, POOL_GROUP_IN, POOL_GROUP_OUT), f32) * POOL_GROUP_IN ** -0.5,
        "pool_scale": 1.0 + 0.1 * n(ks[11], (L, D), f32),
        "w_out": n(ks[12], (L, D, D), f32) * (BETA * D ** -0.5),
        "ln1_g": 1.0 + 0.02 * n(ks[13], (L, D), f32),
        "ln1_b": 0.02 * n(ks[14], (L, D), f32),
        "w_ffn_in": n(ks[15], (L, D, 2 * D_FF), f32) * D ** -0.5,
        "w_ffn_out": n(ks[16], (L, D_FF, D), f32) * (BETA * D_FF ** -0.5),
        "ln2_g": 1.0 + 0.02 * n(ks[17], (L, D), f32),
        "ln2_b": 0.02 * n(ks[18], (L, D), f32),
    }


def reference(x, c, w_ada, b_ada, w_in, lambda_q1, lambda_k1, lambda_q2, lambda_k2, sub_g,
              w_pool, pool_scale, w_out, ln1_g, ln1_b, w_ffn_in, w_ffn_out, ln2_g, ln2_b):
    B, S, D = x.shape
    slopes = jnp.asarray(alibi_slopes(N_HEADS))
    c_act = jax.nn.silu(c)
    for l in range(DEPTH):
        lambda_init = 0.8 - 0.6 * math.exp(-0.3 * l)
        mod = c_act @ w_ada[l] + b_ada[l]
        sh1, sc1, g1, sh2, sc2, g2 = jnp.split(mod, N_MOD, axis=-1)

        h = modulate(layer_norm(x), sh1, sc1)
        proj = h @ w_in[l]
        q, k, v, u, gates = jnp.split(
            proj, np.cumsum([ATTN_QK, ATTN_QK, ATTN_V, POOL_IN]).tolist(), axis=-1)
        q = q.reshape(B, S, N_HEADS, 2, HEAD_DIM)
        k = k.reshape(B, S, N_HEADS, 2, HEAD_DIM)
        v = v.reshape(B, S, N_HEADS, V_DIM)
        lam = (jnp.exp(jnp.sum(lambda_q1[l].astype(jnp.float32) * lambda_k1[l].astype(jnp.float32)))
               - jnp.exp(jnp.sum(lambda_q2[l].astype(jnp.float32) * lambda_k2[l].astype(jnp.float32)))
               + lambda_init)
        a_out = diff_attention(q, k, v, lam, slopes, sub_g[l], lambda_init)
        p_out = multiscale_pool(u, w_pool[l], pool_scale[l])
        ga, gp = jnp.split(gates, N_BRANCHES, axis=-1)
        mixed = jax.nn.sigmoid(ga) * a_out + jax.nn.sigmoid(gp) * p_out
        y = mixed @ w_out[l]
        x = layer_norm(ALPHA * x + g1[:, None, :] * y, ln1_g[l], ln1_b[l])

        h = modulate(layer_norm(x), sh2, sc2)
        gt, up = jnp.split(h @ w_ffn_in[l], 2, axis=-1)
        y = (jax.nn.silu(gt) * up) @ w_ffn_out[l]
        x = layer_norm(ALPHA * x + g2[:, None, :] * y, ln2_g[l], ln2_b[l])
    return x
```

```python
import numpy as np
import concourse.bass as bass
import concourse.mybir as mybir
from concourse.bass_utils import run_bass_kernel_spmd

F32 = mybir.dt.float32
BF16 = mybir.dt.bfloat16
AF = mybir.ActivationFunctionType
ALU = mybir.AluOpType
AX = mybir.AxisListType

D = 1024
NH = 8
DFF = 2816
NFT = DFF // 128
POOL_W = (2, 4, 8, 16)
ALPHA = 2.0 ** 0.25
LN_EPS = 1e-5
RMS_EPS = 1e-5
LAMBDA_INIT = 0.8 - 0.6 * 1.0
QK_SCALE = 0.125
SLOPES = [2.0 ** (-8.0 * (h + 1) / NH) for h in range(NH)]
N_CORES = 8

GR = 512


class Opnd:
    __slots__ = ("ap", "reg")

    def __init__(self, ap, reg):
        self.ap = ap
        self.reg = reg


class Buf:
    def __init__(self, space, ap2d, off_bytes, dims, esz):
        self.space = space
        self.off = off_bytes
        self.dims = tuple(dims)
        self.esz = esz
        n = 1
        strides = []
        for d in reversed(self.dims):
            strides.append(n)
            n *= d
        self.strides = tuple(reversed(strides))
        self.n = n
        self.ap2d = ap2d
        if len(self.dims) == 1:
            self.ap = ap2d
        elif len(self.dims) == 2:
            self.ap = ap2d.rearrange("p (a b) -> p a b", b=self.dims[1])
        elif len(self.dims) == 3:
            self.ap = ap2d.rearrange("p (a b c) -> p a b c", b=self.dims[1], c=self.dims[2])
        else:
            raise ValueError

    def flat(self):
        return Opnd(self.ap2d, (self.space, self.off, self.off + self.n * self.esz))

    def __getitem__(self, idx):
        if not isinstance(idx, tuple):
            idx = (idx,)
        idx = idx + (slice(None),) * (1 + len(self.dims) - len(idx))
        lo = 0
        hi = 0
        for i, d in enumerate(self.dims):
            s = idx[i + 1]
            if isinstance(s, int):
                a, b = s, s + 1
            else:
                a = 0 if s.start is None else s.start
                b = d if s.stop is None else s.stop
            assert 0 <= a < b <= d, (a, b, d)
            lo += a * self.strides[i]
            hi += (b - 1) * self.strides[i]
        hi += 1
        return Opnd(self.ap[idx], (self.space, self.off + lo * self.esz, self.off + hi * self.esz))


class Op:
    __slots__ = ("eng", "fn", "deps", "idx", "needs_inc", "count", "dma", "sem", "semcnt")


class DmaSlot:
    def __init__(self, sem):
        self.sem = sem
        self.cnt = 0


class Prog:
    ENGS = ("pe", "act", "dve", "pool", "sp")

    def __init__(self, nc):
        self.nc = nc
        self.ops = {e: [] for e in self.ENGS}
        self.lastw = {}
        self.readers = {}
        self.waited = {e: {} for e in self.ENGS}
        self.final_dma = []
        self.all_slots = []

    def _grans(self, reg):
        sp, lo, hi = reg
        gr = 2048 if sp == "ps" else GR
        return [(sp, g) for g in range(lo // gr, (hi - 1) // gr + 1)]

    def emit(self, eng, fn, reads=(), writes=(), dma_slot=None):
        op = Op()
        op.eng = eng
        op.fn = fn
        op.idx = len(self.ops[eng])
        op.needs_inc = False
        op.count = None
        op.dma = dma_slot is not None
        if op.dma:
            dma_slot.cnt += 16
            op.sem = dma_slot.sem
            op.semcnt = dma_slot.cnt
        deps = {}
        for reg in reads:
            if reg is None:
                continue
            for g in self._grans(reg):
                w = self.lastw.get(g)
                if w is not None:
                    deps[id(w)] = w
                if reg[0] == "ps":
                    rd = self.readers.get(g)
                    if rd:
                        for e2, r in rd[0].items():
                            if e2 != eng:
                                deps[id(r)] = r
        for reg in writes:
            if reg is None:
                continue
            for g in self._grans(reg):
                w = self.lastw.get(g)
                if w is not None:
                    deps[id(w)] = w
                rd = self.readers.get(g)
                if rd:
                    for r in rd[0].values():
                        deps[id(r)] = r
                    for r in rd[1]:
                        deps[id(r)] = r
        final = []
        best = {}
        for d in deps.values():
            if d.dma:
                final.append(d)
                continue
            if d.eng == eng and eng == "pe" and not op.dma:
                continue
            if self.waited[eng].get(d.eng, -1) >= d.idx:
                continue
            b = best.get(d.eng)
            if b is None or d.idx > b.idx:
                best[d.eng] = d
        for d in best.values():
            d.needs_inc = True
            self.waited[eng][d.eng] = d.idx
            final.append(d)
        op.deps = final
        for reg in writes:
            if reg is None:
                continue
            for g in self._grans(reg):
                self.lastw[g] = op
                self.readers[g] = ({}, [])
        for reg in reads:
            if reg is None:
                continue
            for g in self._grans(reg):
                rd = self.readers.get(g)
                if rd is None:
                    rd = ({}, [])
                    self.readers[g] = rd
                if op.dma:
                    rd[1].append(op)
                else:
                    rd[0][eng] = op
        self.ops[eng].append(op)
        return op

    def replay(self, eng_name, eng, sems):
        dma_waited = {}
        own = sems[eng_name]
        cnt = 0
        for op in self.ops[eng_name]:
            if op.needs_inc and not op.dma:
                cnt += 1
                op.count = cnt
        return cnt

    def run_engine(self, eng_name, eng, sems):
        dma_waited = {}
        sem_waited = {}
        for op in self.ops[eng_name]:
            for d in op.deps:
                if d.dma:
                    key = d.sem.num
                    if dma_waited.get(key, 0) >= d.semcnt:
                        continue
                    dma_waited[key] = d.semcnt
                    eng.wait_ge(d.sem, d.semcnt)
                else:
                    s = sems[d.eng]
                    if sem_waited.get(d.eng, 0) >= d.count:
                        continue
                    sem_waited[d.eng] = d.count
                    eng.wait_ge(s, d.count)
            ins = op.fn(eng)
            if op.dma:
                ins.then_inc(op.sem, 16)
            elif op.needs_inc:
                ins.then_inc(sems[eng_name], 1)
        if eng_name == "sp":
            for slot in self.all_slots:
                if slot.cnt > 0:
                    eng.wait_ge(slot.sem, slot.cnt)


def _make_amat():
    A = np.zeros((128, 4, 5, 128), np.float32)
    for g, w in enumerate(POOL_W):
        half = w // 2
        for j in range(128):
            lo, hi = max(j - half, 0), j + half
            for p in range(lo, min(hi, 128)):
                A[p, g, 0, j] += 1.0 / (hi - lo)
            A[j, g, 0, j] -= 1.0
            lo, hi = j - half, j + half
            for p in range(max(lo, 0), min(hi, 128)):
                A[p, g, 1, j] += 1.0 / w
            A[j, g, 1, j] -= 1.0
            lo, hi = j - half, min(j + half, 128)
            for p in range(max(lo, 0), hi):
                A[p, g, 2, j] += 1.0 / (hi - lo)
            A[j, g, 2, j] -= 1.0
            for p in range(128):
                if p - 128 >= j - half:
                    A[p, g, 3, j] += 1.0 / w
            for p in range(128):
                if p + 128 < j + half:
                    A[p, g, 4, j] += 1.0 / w
    return A


def _make_tables():
    p = np.arange(128, dtype=np.float64)
    ident = np.eye(128, dtype=np.float32)
    bias = np.zeros((128, NH, 32), np.float64)
    fac = np.zeros((128, NH, 4), np.float64)
    for h in range(NH):
        s = SLOPES[h]
        for m in range(16):
            bias[:, h, m] = -s * (128.0 * m - p)
            bias[:, h, 16 + m] = -s * (128.0 * m - 127.0 + p)
        for t in range(2):
            fac[:, h, 2 * t + 0] = np.exp(-s * (t * 128.0 + p))
            fac[:, h, 2 * t + 1] = np.exp(-s * (255.0 - t * 128.0 - p))
    dist = np.zeros((128, 2, 2, 256), np.float32)
    jq = np.arange(256, dtype=np.float64)
    for i in range(2):
        dd = np.abs(jq[None, :] - (i * 128.0 + p[:, None]))
        dist[:, i, 0, :] = dd
        dist[:, i, 1, :] = dd
    return ident, bias.astype(np.float32), fac.astype(np.float32), dist, _make_amat()


def build(T, NSEQ):
    NT = T // 128
    NCH = T // 256
    NC5 = T // 512
    nc = bass.Bass("TRN2", target_bir_lowering=False)

    def din(name, shape):
        return nc.dram_tensor(name, list(shape), F32, kind="ExternalInput").ap()

    x_d = din("x", (NSEQ * T, D))
    cT_d = din("cT", (128, 8 * NSEQ))
    wada_d = din("wada", (6, 128, 8 * 1024))
    badaT_d = din("badaT", (128, 48))
    badag_d = din("badag", (128, 2 * 1024))
    wu_d = din("wu", (128, 8 * 512))
    wh_d = din("wh", (NH, 128, 8 * 640))
    wpool_d = din("wpool", (128, 4 * 256))
    wout_d = din("wout", (128, 8 * 1024))
    wfi_d = din("wfi", (NFT, 128, 8 * 256))
    wfo_d = din("wfo", (128, NFT * 1024))
    lamv_d = din("lamv", (128, 4 * 64))
    subg_d = din("subg", (128, 128))
    pscT_d = din("pscT", (128, 8))
    ln_d = din("lnp", (128, 4 * 1024))
    ident_d = din("ident", (128, 128))
    biast_d = din("biast", (128, NH * 32))
    fact_d = din("fact", (128, NH * 4))
    dist_d = din("dist", (128, 2 * 2 * 256))
    amat_d = din("amat", (128, 4 * 5 * 128))
    out_d = nc.dram_tensor("out", [NSEQ * T, D], F32, kind="ExternalOutput").ap()

    ARENA = 207 * 1024
    arena_cm = nc.sbuf_tensor("arena", [128, ARENA // 2], BF16)
    psum_cm = nc.psum_tensor("ps", [128, 4096], F32)
    arena_h = arena_cm.__enter__()
    psum_h = psum_cm.__enter__()
    arena_bf = arena_h
    arena_f32 = arena_h.bitcast(F32)

    def sb(off, dims, dt):
        n = int(np.prod(dims))
        assert off % 32 == 0
        if dt is F32:
            assert off + n * 4 <= ARENA, (off, n)
            return Buf("sb", arena_f32[:, off // 4: off // 4 + n], off, dims, 4)
        assert off + n * 2 <= ARENA, (off, n)
        return Buf("sb", arena_bf[:, off // 2: off // 2 + n], off, dims, 2)

    psum_bf = psum_h.bitcast(BF16)

    def pbank_bf(b, dims):
        n = int(np.prod(dims))
        assert n <= 1024
        return Buf("ps", psum_bf[:, b * 1024: b * 1024 + n], b * 2048, dims, 2)

    def pbank(b, dims=(512,), nb=1):
        n = int(np.prod(dims))
        assert n <= 512 * nb
        return Buf("ps", psum_h[:, b * 512: b * 512 + n], b * 2048, dims, 4)

    K = 1024

    class Alloc:
        def __init__(self, base, limit):
            self.p = base
            self.limit = limit

        def __call__(self, dims, dt, gran=True):
            n = int(np.prod(dims)) * (4 if dt is F32 else 2)
            off = self.p
            self.p += -(-n // GR) * GR
            assert self.p <= self.limit, ("arena overflow", self.p, self.limit)
            return sb(off, dims, dt)

    pa = Alloc(0, 6 * K)
    ident = pa((128,), F32)
    ones = pa((128,), F32)
    identb = pa((128,), BF16)
    modT = pa((48, NSEQ), F32)
    neglam = pa((1,), F32)
    pscT = pa((8,), F32)
    cT = pa((8, NSEQ), F32)
    cactT = pa((8, NSEQ), F32)
    badaT = pa((48,), F32)
    CACTB = pa((8, NSEQ), BF16)
    P_END = 6 * K

    OFF_MIXED = P_END
    OFF_HT = OFF_MIXED + 32 * K
    OFF_POOLED = OFF_HT + 32 * K
    OFF_H = OFF_POOLED + 16 * K
    mixedT = sb(OFF_MIXED, (8, T), BF16)
    hT = sb(OFF_HT, (8, T), BF16)
    pooledT = sb(OFF_POOLED, (4, T), BF16)
    x1 = sb(OFF_HT, (NT, 1024), F32)
    OFF_X1_END = OFF_HT + 64 * K

    ha = Alloc(OFF_H, ARENA)
    biast = ha((NH, 32), F32)
    fact = ha((NH, 4), F32)
    distt = ha((2, 2, 256), F32)
    amat = ha((4, 5, 128), BF16)
    subg = ha((128,), F32)
    wpool = ha((4, 256), BF16)
    lamv = ha((4, 64), F32)
    lamt = ha((2, 64), F32)
    lams = ha((2,), F32)
    QTZ = [ha((2 * T,), BF16) for _ in range(2)]
    QTZ3 = [Buf("sb", q.ap2d, q.off, (NCH, 2, 256), 2) for q in QTZ]
    TAL = [ha((2, 129), F32) for _ in range(2)]
    KT = [ha((T,), BF16) for _ in range(2)]
    VA = [ha((NT, 144), BF16) for _ in range(2)]
    SGA = [ha((T,), BF16) for _ in range(2)]
    PP = [ha((T,), F32) for _ in range(2)]
    WH = [ha((8, 640), BF16) for _ in range(2)]
    PT = [ha((2, 256), BF16) for _ in range(3)]
    SD = [ha((2, 256), F32) for _ in range(2)]
    SGP = [ha((512,), F32) for _ in range(1)]
    TMPA = [ha((2, 129), F32) for _ in range(1)]
    TMPB = [ha((2, 129), F32) for _ in range(1)]
    TMPC = [ha((2, 129), F32) for _ in range(2)]
    RDEN = [ha((2,), F32) for _ in range(2)]
    NL = [ha((1,), F32) for _ in range(2)]
    O0 = [ha((128,), F32) for _ in range(2)]
    OO = [ha((128,), F32) for _ in range(2)]
    SQ = [ha((128,), F32) for _ in range(2)]
    SS = [ha((1,), F32) for _ in range(2)]
    RSTD = [ha((1,), F32) for _ in range(2)]
    AOUT = [ha((128,), BF16) for _ in range(2)]
    GT1 = [ha((128,), F32) for _ in range(2)]
    H_END = ha.p

    a0 = Alloc(OFF_H + 24 * K, ARENA)
    XT = [a0((1024,), F32) for _ in range(3)]
    XN = [a0((1024,), BF16) for _ in range(2)]
    DBGT = a0((1024,), F32)
    WU = a0((8, 512), BF16)
    UTOK = a0((NT, 512), BF16)
    ST6 = [a0((2, 6), F32) for _ in range(2)]
    MV = [a0((2,), F32) for _ in range(2)]
    RS = [a0((1,), F32) for _ in range(2)]
    NB = [a0((1,), F32) for _ in range(2)]

    ka = Alloc(max(OFF_X1_END, OFF_H), ARENA)
    RS2 = ka((NT,), F32)
    NB2 = ka((NT,), F32)
    CBT = ka((8, 128), BF16)
    BADAG = ka((1024,), F32)
    KEEP_END = ka.p
    ca = Alloc(KEEP_END, ARENA)
    WOUT = ca((8, 1024), BF16)
    PIECE_C = ca((8, 1024), BF16)
    G1T = ca((1024,), F32)
    LN1 = ca((2, 1024), F32)
    CX = [ca((1024,), F32) for _ in range(2)]
    CR = [ca((1024,), F32) for _ in range(2)]
    CST6 = [ca((2, 6), F32) for _ in range(2)]
    CMV = [ca((2,), F32) for _ in range(2)]
    CRS = [ca((1,), F32) for _ in range(2)]
    CNB = [ca((1,), F32) for _ in range(2)]
    CST6b = [ca((2, 6), F32) for _ in range(2)]
    CMVb = [ca((2,), F32) for _ in range(2)]

    fa1 = Alloc(OFF_MIXED, OFF_HT)
    WFI = [fa1((8, 256), BF16) for _ in range(3)]
    H2T = [fa1((8, 512), BF16) for _ in range(1)]
    G2T = fa1((1024,), F32)
    SG = [fa1((512,), F32) for _ in range(2)]
    fa2 = Alloc(KEEP_END, ARENA)
    WFO = fa2((NFT, 1024), BF16)
    ACTT = fa2((NFT, 512), BF16)
    PIECE_F = sb(ACTT.off, (8, 1024), BF16)
    LN2 = fa2((2, 1024), F32)
    FXN = [fa2((1024,), BF16) for _ in range(2)]
    FR_ = [fa2((1024,), F32) for _ in range(2)]
    FST6 = [fa2((2, 6), F32) for _ in range(2)]
    FMV = [fa2((2,), F32) for _ in range(2)]
    FRS = [fa2((1,), F32) for _ in range(2)]
    FNB = [fa2((1,), F32) for _ in range(2)]

    P = Prog(nc)
    n_slots = [0]
    slot_cms = []

    def new_slot():
        cm = nc.semaphore("d%d" % n_slots[0])
        n_slots[0] += 1
        s = cm.__enter__()
        slot_cms.append(cm)
        sl = DmaSlot(s)
        P.all_slots.append(sl)
        return sl

    def regs(*os):
        return [o.reg for o in os if isinstance(o, Opnd)]

    def apof(o):
        return o.ap if isinstance(o, Opnd) else o

    def mm(out, lhsT, rhs, start, stop):
        P.emit("pe", lambda e: e.matmul(out.ap, lhsT.ap, rhs.ap, start=start, stop=stop,
                                        skip_group_check=True),
               reads=regs(lhsT, rhs), writes=regs(out))

    def tr(out, in_):
        P.emit("pe", lambda e: e.matmul(out.ap, in_.ap, identb[:, :].ap, start=True, stop=True,
                                        skip_group_check=True),
               reads=regs(in_, identb[:, :]), writes=regs(out))

    def act(out, in_, func, bias=0.0, scale=1.0, eng="act"):
        P.emit(eng, lambda e: e.activation(out.ap, in_.ap, func, bias=apof(bias), scale=apof(scale)),
               reads=regs(in_, bias, scale), writes=regs(out))

    def ts(out, in0, s1, s2, op0, op1=None, eng="dve"):
        if op1 is None:
            P.emit(eng, lambda e: e.tensor_scalar(out.ap, in0.ap, apof(s1), None, op0),
                   reads=regs(in0, s1), writes=regs(out))
        else:
            P.emit(eng, lambda e: e.tensor_scalar(out.ap, in0.ap, apof(s1), apof(s2), op0, op1),
                   reads=regs(in0, s1, s2), writes=regs(out))

    def tt(out, in0, in1, op, eng="dve"):
        P.emit(eng, lambda e: e.tensor_tensor(out.ap, in0.ap, in1.ap, op),
               reads=regs(in0, in1), writes=regs(out))

    def stt(out, in0, scalar, in1, op0, op1, eng="dve"):
        P.emit(eng, lambda e: e.scalar_tensor_tensor(out.ap, in0.ap, apof(scalar), in1.ap, op0, op1),
               reads=regs(in0, scalar, in1), writes=regs(out))

    def cp(out, in_, eng="dve"):
        if eng == "act":
            P.emit(eng, lambda e: e.copy(out.ap, in_.ap), reads=regs(in_), writes=regs(out))
        else:
            P.emit(eng, lambda e: e.tensor_copy(out.ap, in_.ap), reads=regs(in_), writes=regs(out))

    def memset(out, val, eng="dve"):
        P.emit(eng, lambda e: e.memset(out.ap, val), writes=regs(out))

    def recip(out, in_):
        P.emit("dve", lambda e: e.reciprocal(out.ap, in_.ap), reads=regs(in_), writes=regs(out))

    def rsum(out, in_):
        P.emit("dve", lambda e: e.reduce_sum(out.ap, in_.ap, AX.X), reads=regs(in_), writes=regs(out))

    def bnstats(out, in_):
        P.emit("dve", lambda e: e.bn_stats(out.ap, in_.ap), reads=regs(in_), writes=regs(out))

    def bnaggr(out, in_):
        P.emit("dve", lambda e: e.bn_aggr(out.ap, in_.ap), reads=regs(in_), writes=regs(out))

    auto_slots = {}

    def dma(queue, out, in_, slot, final=False):
        sbside = out if isinstance(out, Opnd) else in_
        key = (queue, sbside.reg[1])
        if key not in auto_slots:
            auto_slots[key] = new_slot()
        slot = auto_slots[key]
        o = P.emit(queue, lambda e: e.dma_start(apof(out), apof(in_)),
                   reads=regs(in_), writes=regs(out), dma_slot=slot)
        if final:
            P.final_dma.append(o)
        return o

    def ln_stats(src, st6, mv, rs, nb):
        bnstats(st6[:, 0, :], Opnd(src.ap[:, 0:512], src.reg))
        bnstats(st6[:, 1, :], Opnd(src.ap[:, 512:1024], src.reg))
        bnaggr(mv[:, :], st6[:, :, :])
        act(rs, mv[:, 1:2], AF.Ln, bias=LN_EPS)
        act(rs, rs, AF.Exp, scale=-0.5)
        stt(nb, mv[:, 0:1], -1.0, rs, ALU.mult, ALU.mult)

    dma("sp", ident[:, :], ident_d, None)
    dma("sp", cT.flat(), cT_d, None)
    dma("sp", badaT[:, :], badaT_d, None)
    dma("sp", pscT[:, :], pscT_d, None)
    memset(ones[:, :], 1.0)
    cp(identb[:, :], ident[:, :])
    act(cactT[:, :, :], cT[:, :, :], AF.Silu)
    cp(CACTB[:, :, :], cactT[:, :, :])
    PIECE0 = [sb(OFF_HT, (8, 1024), BF16), sb(OFF_HT + 16 * K, (8, 1024), BF16)]
    pcount = 0
    bank_rr = [0]

    def next_bank():
        b = bank_rr[0]
        bank_rr[0] = (b + 1) % 8
        return b

    for k in (0, 1, 3, 4):
        pb = PIECE0[pcount % 2]
        dma("pool", pb.flat(), wada_d[k], None)
        pcount += 1
        for nt in range(8):
            bk = pbank(next_bank(), (512,))
            for c in range(8):
                mm(bk[:, 0:NSEQ], pb[:, c, nt * 128:(nt + 1) * 128], CACTB[:, c, :], c == 0, c == 7)
            i = k * 8 + nt
            if k in (1, 4):
                ts(modT[:, i, :], bk[:, 0:NSEQ], badaT[:, i:i + 1], 1.0, ALU.add, ALU.add)
            else:
                ts(modT[:, i, :], bk[:, 0:NSEQ], badaT[:, i:i + 1], None, ALU.add)

    def load_head_consts():
        dma("sp", biast.flat(), biast_d, None)
        dma("sp", fact.flat(), fact_d, None)
        dma("sp", distt.flat(), dist_d, None)
        dma("pool", amat.flat(), amat_d, None)
        dma("sp", subg[:, :], subg_d, None)
        dma("pool", wpool.flat(), wpool_d, None)
        ts(subg[:, :], subg[:, :], (1.0 - LAMBDA_INIT) * (128.0 ** 0.5), None, ALU.mult)

    def init_va():
        for b in range(2):
            memset(VA[b][:, :, 128:144], 1.0)
            memset(QTZ3[b][64:128, :, 0, :], 0.0)
            memset(QTZ3[b][0:64, :, 1, :], 0.0, eng="pool")

    def compute_lambda():
        dma("sp", lamv.flat(), lamv_d, None)
        tt(lamt[:, 0, :], lamv[:, 0, :], lamv[:, 1, :], ALU.mult)
        tt(lamt[:, 1, :], lamv[:, 2, :], lamv[:, 3, :], ALU.mult)
        rsum(lams[:, 0:1], lamt[:, 0, :])
        rsum(lams[:, 1:2], lamt[:, 1, :])
        act(lams[:, :], lams[:, :], AF.Exp)
        stt(neglam[:, :], lams[:, 1:2], -LAMBDA_INIT, lams[:, 0:1], ALU.add, ALU.subtract)

    ASTOP = 0

    def dbg_out(src):
        dma("sp", out_d[0:128, :], src, None, final=True)

    def phase_a0(s):
        r0 = s * T
        dma("pool", WU.flat(), wu_d, None)
        nbuf = 3
        for tt_ in range(min(2, NT)):
            dma("sp", XT[tt_ % nbuf][:, :], x_d[r0 + tt_ * 128: r0 + (tt_ + 1) * 128, :], None)
        for ti in range(NT):
            if ti + 2 < NT:
                dma("sp", XT[(ti + 2) % nbuf][:, :],
                    x_d[r0 + (ti + 2) * 128: r0 + (ti + 3) * 128, :], None)
            xt = XT[ti % nbuf]
            b = ti % 2
            ln_stats(xt[:, :], ST6[b], MV[b], RS[b][:, :], NB[b][:, :])
            if ASTOP == 1:
                cp(DBGT[:, 0:1], RS[b][:, :])
                cp(DBGT[:, 1:2], NB[b][:, :])
                cp(DBGT[:, 2:4], MV[b][:, :])
                dbg_out(DBGT[:, :])
                return
            act(XN[b][:, :], xt[:, :], AF.Identity, bias=NB[b][:, :], scale=RS[b][:, :])
            if ASTOP == 2:
                cp(DBGT[:, :], XN[b][:, :])
                dbg_out(DBGT[:, :])
                return
            for half in range(2):
                bk = pbank(next_bank(), (4, 128))
                for cc in range(4):
                    c = half * 4 + cc
                    if not None:
                        tr(bk[:, cc, :], XN[b][:, c * 128:(c + 1) * 128])
                for cc in range(4):
                    c = half * 4 + cc
                    if None:
                        continue
                    e = "act"
                    if None:
                        cp(hT[:, c, ti * 128:(ti + 1) * 128], bk[:, cc, :], eng=e)
                    elif e == "act":
                        act(hT[:, c, ti * 128:(ti + 1) * 128], bk[:, cc, :], AF.Identity,
                            bias=modT[:, 0 * 8 + c, s:s + 1], scale=modT[:, 1 * 8 + c, s:s + 1])
                    else:
                        ts(hT[:, c, ti * 128:(ti + 1) * 128], bk[:, cc, :],
                           modT[:, 1 * 8 + c, s:s + 1], modT[:, 0 * 8 + c, s:s + 1], ALU.mult, ALU.add)
            if ASTOP == 3:
                memset(DBGT[:, :], 0.0)
                if not None:
                    cp(DBGT[:, 0:128], hT[:, 0, 0:128])
                    cp(DBGT[:, 128:256], hT[:, 7, 0:128])
                dbg_out(DBGT[:, :])
                return
            bk = pbank(next_bank(), (512,))
            for c in range(8):
                mm(bk[:, :], hT[:, c, ti * 128:(ti + 1) * 128], WU[:, c, :], c == 0, c == 7)
            cp(UTOK[:, ti, :], bk[:, :], eng="act" if ti % 2 else "dve")
        for ti in range(NT):
            bk = pbank(next_bank(), (4, 128))
            for g in range(4):
                bands = []
                if ti > 0:
                    bands.append((ti - 1, 3))
                bands.append((ti, 0 if ti == 0 else (2 if ti == NT - 1 else 1)))
                if ti < NT - 1:
                    bands.append((ti + 1, 4))
                for bi, (tsrc, var) in enumerate(bands):
                    mm(bk[:, g, :], UTOK[:, tsrc, g * 128:(g + 1) * 128], amat[:, g, var, :],
                       bi == 0, bi == len(bands) - 1)
            cp(pooledT[:, :, ti * 128:(ti + 1) * 128], bk[:, :, :], eng="act" if ti % 2 else "dve")

    MISC = (6, 7)
    misc_rr = [0]

    def next_misc():
        b = MISC[misc_rr[0] % 2]
        misc_rr[0] += 1
        return b

    def head_proj(s, j, hb):
        w = WH[hb]
        g = j // 2
        for ch in range(NC5):
            tok = slice(ch * 512, (ch + 1) * 512)
            bk = pbank(next_misc(), (2, 256))
            for c in range(8):
                mm(bk.flat(), w[:, c, 0:128], hT[:, c, tok], c == 0, c == 7)
            cp(QTZ3[hb][0:64, 2 * ch:2 * ch + 2, 0, :], bk[0:64, :, :], eng="dve")
            cp(QTZ3[hb][64:128, 2 * ch:2 * ch + 2, 1, :], bk[64:128, :, :], eng="dve")
            bk = pbank(next_misc())
            for c in range(8):
                mm(bk[:, :], w[:, c, 128:256], hT[:, c, tok], c == 0, c == 7)
            cp(KT[hb][:, tok], bk[:, :], eng="act")
            bk = pbank(next_misc(), (4, 128))
            for t4 in range(4):
                ti = ch * 4 + t4
                for c in range(8):
                    mm(bk[:, t4, :], hT[:, c, ti * 128:(ti + 1) * 128], w[:, c, 256:384], c == 0, c == 7)
            cp(VA[hb][:, ch * 4:(ch + 1) * 4, 0:128], bk[:, :, :], eng="dve")
            bk = pbank(next_misc())
            for c in range(8):
                mm(bk[:, :], w[:, c, 384:512], hT[:, c, tok], c == 0, c == 7)
            act(SGA[hb][:, tok], bk[:, :], AF.Sigmoid)
            bk = pbank(next_misc())
            for c in range(8):
                mm(bk[:, :], w[:, c, 512:640], hT[:, c, tok], c == 0, c == 7)
            sg = SGP[0]
            act(sg[:, :], bk[:, :], AF.Sigmoid)
            bk = pbank(next_misc())
            mm(bk[:, :], wpool[:, g, (j % 2) * 128:(j % 2) * 128 + 128], pooledT[:, g, tok], True, True)
            stt(PP[hb][:, tok], bk[:, :], pscT[:, j:j + 1], sg[:, :], ALU.mult, ALU.mult)

    pt_rr = [0]
    sd_rr = [0]
    ep_rr = [0]
    aT_rr = [0]

    ATSTOP = 0

    pending = []

    def run_pending(force=False):
        items = pending[:]
        del pending[:]
        for item in items:
            item[0] -= 1
            if item[0] <= 0 or force:
                item[1]()
            else:
                pending.append(item)

    def flush_pending():
        guard = 0
        while pending and guard < 64:
            run_pending(force=True)
            guard += 1

    def attention(s, j, hb, mid_hook):
        slope = SLOPES[j]
        steps = [(c, kt) for c in range(NCH) for kt in range(NT)]
        nsteps_small = NT < 12

        def emit_qk(i):
            c, kt = steps[i]
            sbk = pbank(i % 3, (512,))
            ksl = slice(kt * 128, (kt + 1) * 128)
            mm(sbk[:, :], KT[hb][:, ksl], QTZ[hb][:, c * 512:(c + 1) * 512], True, True)

        def acc_views():
            base = 3 * 512
            A = [Buf("ps", psum_h[:, base: base + 320], base * 4, (2, 160), 4),
                 Buf("ps", psum_h[:, base + 672: base + 992], (base + 672) * 4, (2, 160), 4)]
            Bv = [Buf("ps", psum_h[:, base + 320: base + 704], (base + 320) * 4, (2, 192), 4),
                  Buf("ps", psum_h[:, base + 1024: base + 1344], (base + 1024) * 4, (2, 160), 4)]
            return A, Bv

        def epilogue(c, t, accs, e):
            hasL = c > 0
            hasR = c < NCH - 1
            Ar, Dr = accs[0][t][:, :, 0:129], accs[1][t][:, :, 0:129]
            FRf = fact[:, j, 2 * t + 1:2 * t + 2]
            ta, tb, tc = TMPA[0], TMPB[0], TMPC[e]
            tal = TAL[t]
            qt = slice((2 * c + t) * 128, (2 * c + t + 1) * 128)

            def s1():
                if hasL and hasR:
                    stt(tb[:, :, :], Ar, FRf, tal[:, :, :], ALU.mult, ALU.add)
                    tt(tc[:, :, :], Dr, tb[:, :, :], ALU.add)
                elif hasL:
                    tt(tc[:, :, :], Dr, tal[:, :, :], ALU.add)
                elif hasR:
                    act(ta[:, :, :], Dr, AF.Identity)
                    stt(tc[:, :, :], Ar, FRf, ta[:, :, :], ALU.mult, ALU.add)
                else:
                    act(tc[:, :, :], Dr, AF.Identity)

            def s2():
                recip(RDEN[e][:, :], tc[:, :, 128])
                ts(NL[e][:, :], RDEN[e][:, 1:2], neglam[:, :], None, ALU.mult)
                ts(O0[e][:, :], tc[:, 0, 0:128], RDEN[e][:, 0:1], None, ALU.mult)
                stt(OO[e][:, :], tc[:, 1, 0:128], NL[e][:, :], O0[e][:, :], ALU.mult, ALU.add)
                tt(SQ[e][:, :], OO[e][:, :], OO[e][:, :], ALU.mult)
                rsum(SS[e][:, :], SQ[e][:, :])

            def s3():
                act(RSTD[e][:, :], SS[e][:, :], AF.Ln, bias=128.0 * RMS_EPS)
                act(RSTD[e][:, :], RSTD[e][:, :], AF.Exp, scale=-0.5)

            def s4():
                stt(AOUT[e][:, :], OO[e][:, :], RSTD[e][:, :], subg[:, :], ALU.mult, ALU.mult)
                slot = aT_rr[0] % 8
                aT_rr[0] += 1
                bk = pbank(MISC[slot // 4], (4, 128))
                tr(bk[:, slot % 4, :], AOUT[e][:, :])

                def s5(bk=bk, slot=slot):
                    tt(GT1[e][:, :], bk[:, slot % 4, :], SGA[hb][:, qt], ALU.mult)
                    tt(mixedT[:, j, qt], GT1[e][:, :], PP[hb][:, qt], ALU.add)
                pending.append([2, s5])

            s1()
            pending.append([1 + t, s2])
            pending.append([3 + t, s3])
            pending.append([5 + t, s4])

        emit_qk(0)
        if len(steps) > 1:
            emit_qk(1)
        accs = acc_views()
        for i, (c, kt) in enumerate(steps):
            if kt == 0:
                if c == NCH // 2 and mid_hook is not None:
                    mid_hook()
            if i + 2 < len(steps):
                emit_qk(i + 2)
            sbk = pbank(i % 3, (2, 256))
            pt = PT[pt_rr[0] % 3]
            pt_rr[0] += 1
            if kt < 2 * c:
                cat = 0
                bidx = 2 * c - kt
                act(pt.flat(), sbk.flat(), AF.Exp, bias=biast[:, j, bidx:bidx + 1], scale=QK_SCALE)
            elif kt > 2 * c + 1:
                cat = 2
                bidx = 16 + (kt - 2 * c - 1)
                act(pt.flat(), sbk.flat(), AF.Exp, bias=biast[:, j, bidx:bidx + 1], scale=QK_SCALE)
            else:
                cat = 1
                ii = kt - 2 * c
                sd = SD[sd_rr[0] % 2]
                sd_rr[0] += 1
                stt(sd[:, :, :], distt[:, ii, :, :], -slope / QK_SCALE, sbk[:, :, :], ALU.mult, ALU.add)
                act(pt.flat(), sd.flat(), AF.Exp, scale=QK_SCALE)
            if cat == 0:
                first, last = kt == 0, kt == 2 * c - 1
            elif cat == 1:
                first, last = kt == 2 * c, kt == 2 * c + 1
            else:
                first, last = kt == 2 * c + 2, kt == NT - 1
            seen = set()
            for t in range(2):
                dst = accs[1][t] if cat == 1 else accs[0][t]
                for m in range(2):
                    if cat == 1:
                        bank = 3 if (t == 0 and m == 0) else (4 if t == 0 else 5)
                    else:
                        bank = 3 if t == 0 else 4
                    st = first and (bank not in seen)
                    seen.add(bank)
                    mm(dst[:, m, 0:132], pt[:, m, t * 128:(t + 1) * 128], VA[hb][:, kt, 0:132], st, last)
            if cat == 0 and last:
                for t in range(2):
                    act(TAL[t][:, :, :], accs[0][t][:, :, 0:129], AF.Identity, scale=fact[:, j, 2 * t:2 * t + 1])
            run_pending()
            if kt == NT - 1:
                if NT < 12:
                    flush_pending()
                for t in range(2):
                    e = ep_rr[0] % 2
                    ep_rr[0] += 1
                    epilogue(c, t, accs, e)
        if nsteps_small:
            flush_pending()
        return False

    def bcast_mod_tile(s, k, piece, gtile, bidx):
        dma("pool", piece.flat(), wada_d[k], None)
        dma("sp", BADAG[:, :], badag_d[:, bidx * 1024:(bidx + 1) * 1024], None)
        for c in range(8):
            act(CBT[:, c, :], ones[:, :], AF.Identity, scale=cactT[:, c, s:s + 1])
        for half in range(2):
            bk = pbank(next_bank())
            for c in range(8):
                mm(bk[:, :], CBT[:, c, :], piece[:, c, half * 512:(half + 1) * 512], c == 0, c == 7)
            tt(gtile[:, half * 512:(half + 1) * 512], bk[:, :], BADAG[:, half * 512:(half + 1) * 512], ALU.add)

    def phase_c(s):
        r0 = s * T
        dma("pool", WOUT.flat(), wout_d, None)
        dma("sp", LN1.flat(), ln_d[:, 0:2048], None)
        bcast_mod_tile(s, 2, PIECE_C, G1T, 0)
        dma("sp", CX[0][:, :], x_d[r0: r0 + 128, :], None)
        for ti in range(NT):
            b = ti % 2
            if ti + 1 < NT:
                dma("sp", CX[1 - b][:, :], x_d[r0 + (ti + 1) * 128: r0 + (ti + 2) * 128, :], None)
            r = CR[b]
            for half in range(2):
                bk = pbank(next_bank())
                hs = slice(half * 512, (half + 1) * 512)
                for c in range(8):
                    mm(bk[:, :], mixedT[:, c, ti * 128:(ti + 1) * 128], WOUT[:, c, hs], c == 0, c == 7)
                tt(r[:, hs], bk[:, :], G1T[:, hs], ALU.mult)
            stt(r[:, :], CX[b][:, :], ALPHA, r[:, :], ALU.mult, ALU.add)
            ln_stats(r[:, :], CST6[b], CMV[b], CRS[b][:, :], CNB[b][:, :])
            act(r[:, :], r[:, :], AF.Identity, bias=CNB[b][:, :], scale=CRS[b][:, :])
            tt(r[:, :], r[:, :], LN1[:, 0, :], ALU.mult)
            tt(x1[:, ti, :], r[:, :], LN1[:, 1, :], ALU.add, eng="pool")
            ln_stats(x1[:, ti, :], CST6b[b], CMVb[b], RS2[:, ti:ti + 1], NB2[:, ti:ti + 1])

    def phase_ffn(s):
        r0 = s * T
        dma("pool", WFO.flat(), wfo_d, None)
        dma("sp", LN2.flat(), ln_d[:, 2048:4096], None)
        bcast_mod_tile(s, 5, PIECE_F, G2T, 1)
        nw = 3
        wcount = [0]

        def issue_wfi(ft):
            k_ = wcount[0] % nw
            wcount[0] += 1
            dma("pool", WFI[k_].flat(), wfi_d[ft], None)
            return k_

        for ch in range(NC5):
            h2 = H2T[0]
            pend = [issue_wfi(0), issue_wfi(1)]
            for t4 in range(4):
                ti = ch * 4 + t4
                b = ti % 2
                act(FXN[b][:, :], x1[:, ti, :], AF.Identity, bias=NB2[:, ti:ti + 1], scale=RS2[:, ti:ti + 1])
                for half in range(2):
                    bk = pbank(next_bank(), (4, 128))
                    for cc in range(4):
                        c = half * 4 + cc
                        tr(bk[:, cc, :], FXN[b][:, c * 128:(c + 1) * 128])
                    for cc in range(4):
                        c = half * 4 + cc
                        if True:
                            act(h2[:, c, t4 * 128:(t4 + 1) * 128], bk[:, cc, :], AF.Identity,
                                bias=modT[:, 3 * 8 + c, s:s + 1], scale=modT[:, 4 * 8 + c, s:s + 1])
                        else:
                            ts(h2[:, c, t4 * 128:(t4 + 1) * 128], bk[:, cc, :],
                               modT[:, 4 * 8 + c, s:s + 1], modT[:, 3 * 8 + c, s:s + 1], ALU.mult, ALU.add)
            for ft in range(NFT):
                wk = WFI[pend.pop(0)]
                if ft + 2 < NFT:
                    pend.append(issue_wfi(ft + 2))
                bg = pbank(next_bank())
                for c in range(8):
                    mm(bg[:, :], wk[:, c, 0:128], h2[:, c, :], c == 0, c == 7)
                bu = pbank(next_bank())
                for c in range(8):
                    mm(bu[:, :], wk[:, c, 128:256], h2[:, c, :], c == 0, c == 7)
                sg = SG[ft % 2]
                act(sg[:, :], bg[:, :], AF.Silu)
                tt(ACTT[:, ft, :], bu[:, :], sg[:, :], ALU.mult)
            for t4 in range(4):
                ti = ch * 4 + t4
                b = ti % 2
                r = FR_[b]
                for half in range(2):
                    bk = pbank(next_bank())
                    hs = slice(half * 512, (half + 1) * 512)
                    for ft in range(NFT):
                        mm(bk[:, :], ACTT[:, ft, t4 * 128:(t4 + 1) * 128], WFO[:, ft, hs], ft == 0, ft == NFT - 1)
                    tt(r[:, hs], bk[:, :], G2T[:, hs], ALU.mult)
                stt(r[:, :], x1[:, ti, :], ALPHA, r[:, :], ALU.mult, ALU.add)
                ln_stats(r[:, :], FST6[b], FMV[b], FRS[b][:, :], FNB[b][:, :])
                act(r[:, :], r[:, :], AF.Identity, bias=FNB[b][:, :], scale=FRS[b][:, :])
                tt(r[:, :], r[:, :], LN2[:, 0, :], ALU.mult)
                tt(r[:, :], r[:, :], LN2[:, 1, :], ALU.add, eng="pool")
                dma("sp", out_d[r0 + ti * 128: r0 + (ti + 1) * 128, :], r[:, :], None, final=True)

    compute_lambda_done = [False]
    KSTOP = 0

    def dbg_out(src):
        dma("sp", out_d[0:128, :], src, None, final=True)

    MICRO = 0
    for s in range(NSEQ if not MICRO else 0):
        if KSTOP == 1:
            cp(DBGT[:, 0:48 * NSEQ], Opnd(modT.ap2d, modT.flat().reg))
            dbg_out(DBGT[:, :])
            break
        load_head_consts()
        if not compute_lambda_done[0]:
            compute_lambda()
            compute_lambda_done[0] = True
        if KSTOP == 2:
            cp(DBGT[:, 0:1], neglam[:, :])
            cp(DBGT[:, 128:256], subg[:, :])
            dbg_out(DBGT[:, :])
            break
        phase_a0(s)
        if ASTOP:
            break
        init_va()
        if KSTOP == 3:
            cp(DBGT[:, 0:512], pooledT[:, 0, 0:512])
            cp(DBGT[:, 512:1024], hT[:, 0, 0:512])
            dbg_out(DBGT[:, :])
            break
        dma("pool", WH[0].flat(), wh_d[0], None)
        head_proj(s, 0, 0)
        if KSTOP == 4:
            cp(DBGT[:, 0:512], QTZ[0][:, 0:512])
            cp(DBGT[:, 512:1024], PP[0][:, 0:512])
            dbg_out(DBGT[:, :])
            break
        for j in range(NH):
            hb = j % 2
            hook = None
            if j + 1 < NH:
                dma("pool", WH[1 - hb].flat(), wh_d[j + 1], None)
                hook = (lambda jj=j + 1, bb=1 - hb: head_proj(s, jj, bb))
            if attention(s, j, hb, hook):
                break
            if hook is not None and NCH // 2 == 0:
                hook()
            if KSTOP == 5:
                break
        if ATSTOP:
            break
        if KSTOP in (5, 6):
            cp(DBGT[:, 0:512], mixedT[:, 0, 0:512])
            cp(DBGT[:, 512:1024], mixedT[:, 7, 0:512])
            dbg_out(DBGT[:, :])
            break
        flush_pending()
        phase_c(s)
        if KSTOP == 7:
            dbg_out(x1[:, 0, :])
            break
        phase_ffn(s)

    sem_cms = {e: nc.semaphore("s_" + e) for e in ("pe", "act", "dve", "pool")}
    sems = {e: cm.__enter__() for e, cm in sem_cms.items()}
    for e in ("pe", "act", "dve", "pool"):
        P.replay(e, None, sems)
    with nc.Block() as block:
        @block.tensor
        def _(eng):
            P.run_engine("pe", eng, sems)

        @block.scalar
        def _(eng):
            P.run_engine("act", eng, sems)

        @block.vector
        def _(eng):
            P.run_engine("dve", eng, sems)

        @block.gpsimd
        def _(eng):
            P.run_engine("pool", eng, sems)

        @block.sync
        def _(eng):
            P.run_engine("sp", eng, sems)
    return nc, P


def _chunk_rows(w):
    k, n = w.shape
    return np.ascontiguousarray(w.reshape(k // 128, 128, n).transpose(1, 0, 2))


def prep_shared(inp):
    f = lambda a: np.ascontiguousarray(np.asarray(a, dtype=np.float32))
    w_ada = f(inp["w_ada"])[0]
    b_ada = f(inp["b_ada"])[0]
    w_in = f(inp["w_in"])[0]
    sh = {}
    sh["wada"] = np.ascontiguousarray(
        np.stack([_chunk_rows(w_ada[:, k * 1024:(k + 1) * 1024]).reshape(128, 8 * 1024) for k in range(6)]))
    sh["badaT"] = np.ascontiguousarray(b_ada.reshape(48, 128).T)
    sh["badag"] = np.ascontiguousarray(
        np.broadcast_to(np.concatenate([b_ada[2048:3072], b_ada[5120:6144]])[None, :], (128, 2048)))
    sh["wu"] = _chunk_rows(w_in[:, 3072:3584]).reshape(128, 8 * 512)
    wh = []
    for j in range(NH):
        cols = np.concatenate([
            w_in[:, j * 128:(j + 1) * 128],
            w_in[:, 1024 + j * 128:1024 + (j + 1) * 128],
            w_in[:, 2048 + j * 128:2048 + (j + 1) * 128],
            w_in[:, 3584 + j * 128:3584 + (j + 1) * 128],
            w_in[:, 4608 + j * 128:4608 + (j + 1) * 128]], axis=1)
        wh.append(_chunk_rows(cols).reshape(128, 8 * 640))
    sh["wh"] = np.ascontiguousarray(np.stack(wh))
    sh["wpool"] = np.ascontiguousarray(f(inp["w_pool"])[0].transpose(1, 0, 2)).reshape(128, 4 * 256)
    sh["wout"] = _chunk_rows(f(inp["w_out"])[0]).reshape(128, 8 * 1024)
    wfi_full = f(inp["w_ffn_in"])[0]
    wfi = []
    for ft in range(NFT):
        cols = np.concatenate([wfi_full[:, ft * 128:(ft + 1) * 128],
                               wfi_full[:, DFF + ft * 128:DFF + (ft + 1) * 128]], axis=1)
        wfi.append(_chunk_rows(cols).reshape(128, 8 * 256))
    sh["wfi"] = np.ascontiguousarray(np.stack(wfi))
    sh["wfo"] = _chunk_rows(f(inp["w_ffn_out"])[0]).reshape(128, NFT * 1024)
    lam = np.stack([f(inp["lambda_q1"])[0], f(inp["lambda_k1"])[0],
                    f(inp["lambda_q2"])[0], f(inp["lambda_k2"])[0]]).reshape(1, 256)
    sh["lamv"] = np.ascontiguousarray(np.broadcast_to(lam, (128, 256)))
    sh["subg"] = np.ascontiguousarray(np.broadcast_to(f(inp["sub_g"])[0][None, :], (128, 128)))
    sh["pscT"] = np.ascontiguousarray(f(inp["pool_scale"])[0].reshape(8, 128).T)
    lnp = np.concatenate([f(inp["ln1_g"])[0], f(inp["ln1_b"])[0], f(inp["ln2_g"])[0], f(inp["ln2_b"])[0]])
    sh["lnp"] = np.ascontiguousarray(np.broadcast_to(lnp[None, :], (128, 4096)))
    ident, bias, fac, dist, amat = _make_tables()
    sh["ident"] = ident
    sh["biast"] = bias.reshape(128, NH * 32)
    sh["fact"] = fac.reshape(128, NH * 4)
    sh["dist"] = dist.reshape(128, 1024)
    sh["amat"] = amat.reshape(128, 4 * 5 * 128)
    return sh


def prep_core(x, c, core, T, NSEQ):
    xs = np.ascontiguousarray(x[core * NSEQ:(core + 1) * NSEQ].reshape(NSEQ * T, D), dtype=np.float32)
    cs = np.asarray(c[core * NSEQ:(core + 1) * NSEQ], dtype=np.float32)
    cT = np.ascontiguousarray(cs.T.reshape(8, 128, NSEQ).transpose(1, 0, 2)).reshape(128, 8 * NSEQ)
    return {"x": xs, "cT": cT}


def kernel(**inputs):
    x = np.asarray(inputs["x"], dtype=np.float32)
    c = np.asarray(inputs["c"], dtype=np.float32)
    B, T, _ = x.shape
    NSEQ = B // N_CORES
    nc, _ = build(T, NSEQ)
    sh = prep_shared(inputs)
    in_maps = []
    for core in range(N_CORES):
        m = dict(sh)
        m.update(prep_core(x, c, core, T, NSEQ))
        in_maps.append(m)
    res = run_bass_kernel_spmd(nc, in_maps, core_ids=list(range(N_CORES)))
    outs = [np.asarray(r["out"], dtype=np.float32).reshape(NSEQ, T, D) for r in res.results]
    return np.concatenate(outs, axis=0)
```

```python
import numpy as np
import concourse.bass as bass
import concourse.mybir as mybir
from concourse.bass_utils import run_bass_kernel_spmd

F32 = mybir.dt.float32
BF16 = mybir.dt.bfloat16
AF = mybir.ActivationFunctionType
ALU = mybir.AluOpType
AX = mybir.AxisListType

D = 1024
NH = 8
DFF = 2816
NFT = DFF // 128
POOL_W = (2, 4, 8, 16)
ALPHA = 2.0 ** 0.25
LN_EPS = 1e-5
RMS_EPS = 1e-5
LAMBDA_INIT = 0.8 - 0.6 * 1.0
QK_SCALE = 0.125
SLOPES = [2.0 ** (-8.0 * (h + 1) / NH) for h in range(NH)]
N_CORES = 8

GR = 512


class Opnd:
    __slots__ = ("ap", "reg")

    def __init__(self, ap, reg):
        self.ap = ap
        self.reg = reg


class Buf:
    def __init__(self, space, ap2d, off_bytes, dims, esz):
        self.space = space
        self.off = off_bytes
        self.dims = tuple(dims)
        self.esz = esz
        n = 1
        strides = []
        for d in reversed(self.dims):
            strides.append(n)
            n *= d
        self.strides = tuple(reversed(strides))
        self.n = n
        self.ap2d = ap2d
        if len(self.dims) == 1:
            self.ap = ap2d
        elif len(self.dims) == 2:
            self.ap = ap2d.rearrange("p (a b) -> p a b", b=self.dims[1])
        elif len(self.dims) == 3:
            self.ap = ap2d.rearrange("p (a b c) -> p a b c", b=self.dims[1], c=self.dims[2])
        else:
            raise ValueError

    def flat(self):
        return Opnd(self.ap2d, (self.space, self.off, self.off + self.n * self.esz))

    def __getitem__(self, idx):
        if not isinstance(idx, tuple):
            idx = (idx,)
        idx = idx + (slice(None),) * (1 + len(self.dims) - len(idx))
        lo = 0
        hi = 0
        for i, d in enumerate(self.dims):
            s = idx[i + 1]
            if isinstance(s, int):
                a, b = s, s + 1
            else:
                a = 0 if s.start is None else s.start
                b = d if s.stop is None else s.stop
            assert 0 <= a < b <= d, (a, b, d)
            lo += a * self.strides[i]
            hi += (b - 1) * self.strides[i]
        hi += 1
        return Opnd(self.ap[idx], (self.space, self.off + lo * self.esz, self.off + hi * self.esz))


class Op:
    __slots__ = ("eng", "fn", "deps", "idx", "needs_inc", "count", "dma", "sem", "semcnt")


class DmaSlot:
    def __init__(self, sem):
        self.sem = sem
        self.cnt = 0


class Prog:
    ENGS = ("pe", "act", "dve", "pool", "sp")

    def __init__(self, nc):
        self.nc = nc
        self.ops = {e: [] for e in self.ENGS}
        self.lastw = {}
        self.readers = {}
        self.waited = {e: {} for e in self.ENGS}
        self.final_dma = []
        self.all_slots = []

    def _grans(self, reg):
        sp, lo, hi = reg
        gr = 2048 if sp == "ps" else GR
        return [(sp, g) for g in range(lo // gr, (hi - 1) // gr + 1)]

    def emit(self, eng, fn, reads=(), writes=(), dma_slot=None):
        op = Op()
        op.eng = eng
        op.fn = fn
        op.idx = len(self.ops[eng])
        op.needs_inc = False
        op.count = None
        op.dma = dma_slot is not None
        if op.dma:
            dma_slot.cnt += 16
            op.sem = dma_slot.sem
            op.semcnt = dma_slot.cnt
        deps = {}
        for reg in reads:
            if reg is None:
                continue
            for g in self._grans(reg):
                w = self.lastw.get(g)
                if w is not None:
                    deps[id(w)] = w
                if reg[0] == "ps":
                    rd = self.readers.get(g)
                    if rd:
                        for e2, r in rd[0].items():
                            if e2 != eng:
                                deps[id(r)] = r
        for reg in writes:
            if reg is None:
                continue
            for g in self._grans(reg):
                w = self.lastw.get(g)
                if w is not None:
                    deps[id(w)] = w
                rd = self.readers.get(g)
                if rd:
                    for r in rd[0].values():
                        deps[id(r)] = r
                    for r in rd[1]:
                        deps[id(r)] = r
        final = []
        best = {}
        for d in deps.values():
            if d.dma:
                final.append(d)
                continue
            if d.eng == eng and eng == "pe" and not op.dma:
                continue
            if self.waited[eng].get(d.eng, -1) >= d.idx:
                continue
            b = best.get(d.eng)
            if b is None or d.idx > b.idx:
                best[d.eng] = d
        for d in best.values():
            d.needs_inc = True
            self.waited[eng][d.eng] = d.idx
            final.append(d)
        op.deps = final
        for reg in writes:
            if reg is None:
                continue
            for g in self._grans(reg):
                self.lastw[g] = op
                self.readers[g] = ({}, [])
        for reg in reads:
            if reg is None:
                continue
            for g in self._grans(reg):
                rd = self.readers.get(g)
                if rd is None:
                    rd = ({}, [])
                    self.readers[g] = rd
                if op.dma:
                    rd[1].append(op)
                else:
                    rd[0][eng] = op
        self.ops[eng].append(op)
        return op

    def replay(self, eng_name, eng, sems):
        dma_waited = {}
        own = sems[eng_name]
        cnt = 0
        for op in self.ops[eng_name]:
            if op.needs_inc and not op.dma:
                cnt += 1
                op.count = cnt
        return cnt

    def run_engine(self, eng_name, eng, sems):
        dma_waited = {}
        sem_waited = {}
        for op in self.ops[eng_name]:
            for d in op.deps:
                if d.dma:
                    key = d.sem.num
                    if dma_waited.get(key, 0) >= d.semcnt:
                        continue
                    dma_waited[key] = d.semcnt
                    eng.wait_ge(d.sem, d.semcnt)
                else:
                    s = sems[d.eng]
                    if sem_waited.get(d.eng, 0) >= d.count:
                        continue
                    sem_waited[d.eng] = d.count
                    eng.wait_ge(s, d.count)
            ins = op.fn(eng)
            if op.dma:
                ins.then_inc(op.sem, 16)
            elif op.needs_inc:
                ins.then_inc(sems[eng_name], 1)
        if eng_name == "sp":
            for slot in self.all_slots:
                if slot.cnt > 0:
                    eng.wait_ge(slot.sem, slot.cnt)


def _make_amat():
    A = np.zeros((128, 4, 5, 128), np.float32)
    for g, w in enumerate(POOL_W):
        half = w // 2
        for j in range(128):
            lo, hi = max(j - half, 0), j + half
            for p in range(lo, min(hi, 128)):
                A[p, g, 0, j] += 1.0 / (hi - lo)
            A[j, g, 0, j] -= 1.0
            lo, hi = j - half, j + half
            for p in range(max(lo, 0), min(hi, 128)):
                A[p, g, 1, j] += 1.0 / w
            A[j, g, 1, j] -= 1.0
            lo, hi = j - half, min(j + half, 128)
            for p in range(max(lo, 0), hi):
                A[p, g, 2, j] += 1.0 / (hi - lo)
            A[j, g, 2, j] -= 1.0
            for p in range(128):
                if p - 128 >= j - half:
                    A[p, g, 3, j] += 1.0 / w
            for p in range(128):
                if p + 128 < j + half:
                    A[p, g, 4, j] += 1.0 / w
    return A


def _make_tables():
    p = np.arange(128, dtype=np.float64)
    ident = np.eye(128, dtype=np.float32)
    bias = np.zeros((128, NH, 32), np.float64)
    fac = np.zeros((128, NH, 4), np.float64)
    for h in range(NH):
        s = SLOPES[h]
        for m in range(16):
            bias[:, h, m] = -s * (128.0 * m - p)
            bias[:, h, 16 + m] = -s * (128.0 * m - 127.0 + p)
        for t in range(2):
            fac[:, h, 2 * t + 0] = np.exp(-s * (t * 128.0 + p))
            fac[:, h, 2 * t + 1] = np.exp(-s * (255.0 - t * 128.0 - p))
    dist = np.zeros((128, 2, 2, 256), np.float32)
    jq = np.arange(256, dtype=np.float64)
    for i in range(2):
        dd = np.abs(jq[None, :] - (i * 128.0 + p[:, None]))
        dist[:, i, 0, :] = dd
        dist[:, i, 1, :] = dd
    return ident, bias.astype(np.float32), fac.astype(np.float32), dist, _make_amat()


def build(T, NSEQ):
    NT = T // 128
    NCH = T // 256
    NC5 = T // 512
    nc = bass.Bass("TRN2", target_bir_lowering=False)

    def din(name, shape):
        return nc.dram_tensor(name, list(shape), F32, kind="ExternalInput").ap()

    x_d = din("x", (NSEQ * T, D))
    cT_d = din("cT", (128, 8 * NSEQ))
    wada_d = din("wada", (6, 128, 8 * 1024))
    badaT_d = din("badaT", (128, 48))
    badag_d = din("badag", (128, 2 * 1024))
    wu_d = din("wu", (128, 8 * 512))
    wh_d = din("wh", (NH, 128, 8 * 640))
    wpool_d = din("wpool", (128, 4 * 256))
    wout_d = din("wout", (128, 8 * 1024))
    wfi_d = din("wfi", (NFT, 128, 8 * 256))
    wfo_d = din("wfo", (128, NFT * 1024))
    lamv_d = din("lamv", (128, 4 * 64))
    subg_d = din("subg", (128, 128))
    pscT_d = din("pscT", (128, 8))
    ln_d = din("lnp", (128, 4 * 1024))
    ident_d = din("ident", (128, 128))
    biast_d = din("biast", (128, NH * 32))
    fact_d = din("fact", (128, NH * 4))
    dist_d = din("dist", (128, 2 * 2 * 256))
    amat_d = din("amat", (128, 4 * 5 * 128))
    out_d = nc.dram_tensor("out", [NSEQ * T, D], F32, kind="ExternalOutput").ap()

    ARENA = 207 * 1024
    arena_cm = nc.sbuf_tensor("arena", [128, ARENA // 2], BF16)
    psum_cm = nc.psum_tensor("ps", [128, 4096], F32)
    arena_h = arena_cm.__enter__()
    psum_h = psum_cm.__enter__()
    arena_bf = arena_h
    arena_f32 = arena_h.bitcast(F32)

    def sb(off, dims, dt):
        n = int(np.prod(dims))
        assert off % 32 == 0
        if dt is F32:
            assert off + n * 4 <= ARENA, (off, n)
            return Buf("sb", arena_f32[:, off // 4: off // 4 + n], off, dims, 4)
        assert off + n * 2 <= ARENA, (off, n)
        return Buf("sb", arena_bf[:, off // 2: off // 2 + n], off, dims, 2)

    psum_bf = psum_h.bitcast(BF16)

    def pbank_bf(b, dims):
        n = int(np.prod(dims))
        assert n <= 1024
        return Buf("ps", psum_bf[:, b * 1024: b * 1024 + n], b * 2048, dims, 2)

    def pbank(b, dims=(512,), nb=1):
        n = int(np.prod(dims))
        assert n <= 512 * nb
        return Buf("ps", psum_h[:, b * 512: b * 512 + n], b * 2048, dims, 4)

    K = 1024

    class Alloc:
        def __init__(self, base, limit):
            self.p = base
            self.limit = limit

        def __call__(self, dims, dt, gran=True):
            n = int(np.prod(dims)) * (4 if dt is F32 else 2)
            off = self.p
            self.p += -(-n // GR) * GR
            assert self.p <= self.limit, ("arena overflow", self.p, self.limit)
            return sb(off, dims, dt)

    pa = Alloc(0, 6 * K)
    ident = pa((128,), F32)
    ones = pa((128,), F32)
    identb = pa((128,), BF16)
    modT = pa((48, NSEQ), F32)
    neglam = pa((1,), F32)
    pscT = pa((8,), F32)
    cT = pa((8, NSEQ), F32)
    cactT = pa((8, NSEQ), F32)
    badaT = pa((48,), F32)
    CACTB = pa((8, NSEQ), BF16)
    P_END = 6 * K

    OFF_MIXED = P_END
    OFF_HT = OFF_MIXED + 32 * K
    OFF_POOLED = OFF_HT + 32 * K
    OFF_H = OFF_POOLED + 16 * K
    mixedT = sb(OFF_MIXED, (8, T), BF16)
    hT = sb(OFF_HT, (8, T), BF16)
    pooledT = sb(OFF_POOLED, (4, T), BF16)
    x1 = sb(OFF_HT, (NT, 1024), F32)
    OFF_X1_END = OFF_HT + 64 * K

    ha = Alloc(OFF_H, ARENA)
    biast = ha((NH, 32), F32)
    fact = ha((NH, 4), F32)
    distt = ha((2, 2, 256), F32)
    amat = ha((4, 5, 128), BF16)
    subg = ha((128,), F32)
    wpool = ha((4, 256), BF16)
    lamv = ha((4, 64), F32)
    lamt = ha((2, 64), F32)
    lams = ha((2,), F32)
    QTZ = [ha((2 * T,), BF16) for _ in range(2)]
    QTZ3 = [Buf("sb", q.ap2d, q.off, (NCH, 2, 256), 2) for q in QTZ]
    TAL = [ha((2, 129), F32) for _ in range(2)]
    KT = [ha((T,), BF16) for _ in range(2)]
    VA = [ha((NT, 144), BF16) for _ in range(2)]
    SGA = [ha((T,), BF16) for _ in range(2)]
    PP = [ha((T,), F32) for _ in range(2)]
    WH = [ha((8, 640), BF16) for _ in range(2)]
    PT = [ha((2, 256), BF16) for _ in range(3)]
    SD = [ha((2, 256), F32) for _ in range(2)]
    SGP = [ha((512,), F32) for _ in range(1)]
    TMPA = [ha((2, 129), F32) for _ in range(1)]
    TMPB = [ha((2, 129), F32) for _ in range(2)]
    TMPC = [ha((2, 129), F32) for _ in range(2)]
    RDEN = [ha((2,), F32) for _ in range(2)]
    NL = [ha((1,), F32) for _ in range(2)]
    O0 = [ha((128,), F32) for _ in range(2)]
    OO = [ha((128,), F32) for _ in range(2)]
    SQ = [ha((128,), F32) for _ in range(2)]
    SS = [ha((1,), F32) for _ in range(2)]
    RSTD = [ha((1,), F32) for _ in range(2)]
    AOUT = [ha((128,), BF16) for _ in range(2)]
    GT1 = [ha((128,), F32) for _ in range(2)]
    H_END = ha.p

    a0 = Alloc(OFF_H + 24 * K, ARENA)
    XT = [a0((1024,), F32) for _ in range(3)]
    XN = [a0((1024,), BF16) for _ in range(2)]
    DBGT = a0((1024,), F32)
    WU = a0((8, 512), BF16)
    UTOK = a0((NT, 512), BF16)
    ST6 = [a0((2, 6), F32) for _ in range(2)]
    MV = [a0((2,), F32) for _ in range(2)]
    RS = [a0((1,), F32) for _ in range(2)]
    NB = [a0((1,), F32) for _ in range(2)]

    ka = Alloc(max(OFF_X1_END, OFF_H), ARENA)
    RS2 = ka((NT,), F32)
    NB2 = ka((NT,), F32)
    CBT = ka((8, 128), BF16)
    BADAG = ka((1024,), F32)
    KEEP_END = ka.p
    ca = Alloc(KEEP_END, ARENA)
    WOUT = ca((8, 1024), BF16)
    PIECE_C = ca((8, 1024), BF16)
    G1T = ca((1024,), F32)
    LN1 = ca((2, 1024), F32)
    CX = [ca((1024,), F32) for _ in range(2)]
    CR = [ca((1024,), F32) for _ in range(2)]
    CST6 = [ca((2, 6), F32) for _ in range(2)]
    CMV = [ca((2,), F32) for _ in range(2)]
    CRS = [ca((1,), F32) for _ in range(2)]
    CNB = [ca((1,), F32) for _ in range(2)]
    CST6b = [ca((2, 6), F32) for _ in range(2)]
    CMVb = [ca((2,), F32) for _ in range(2)]

    fa1 = Alloc(OFF_MIXED, OFF_HT)
    WFI = [fa1((8, 256), BF16) for _ in range(3)]
    H2T = [fa1((8, 512), BF16) for _ in range(1)]
    G2T = fa1((1024,), F32)
    SG = [fa1((512,), F32) for _ in range(2)]
    fa2 = Alloc(KEEP_END, ARENA)
    WFO = fa2((NFT, 1024), BF16)
    ACTT = fa2((NFT, 512), BF16)
    PIECE_F = sb(ACTT.off, (8, 1024), BF16)
    LN2 = fa2((2, 1024), F32)
    FXN = [fa2((1024,), BF16) for _ in range(2)]
    FR_ = [fa2((1024,), F32) for _ in range(2)]
    FST6 = [fa2((2, 6), F32) for _ in range(2)]
    FMV = [fa2((2,), F32) for _ in range(2)]
    FRS = [fa2((1,), F32) for _ in range(2)]
    FNB = [fa2((1,), F32) for _ in range(2)]

    P = Prog(nc)
    n_slots = [0]
    slot_cms = []

    def new_slot():
        cm = nc.semaphore("d%d" % n_slots[0])
        n_slots[0] += 1
        s = cm.__enter__()
        slot_cms.append(cm)
        sl = DmaSlot(s)
        P.all_slots.append(sl)
        return sl

    def regs(*os):
        return [o.reg for o in os if isinstance(o, Opnd)]

    def apof(o):
        return o.ap if isinstance(o, Opnd) else o

    def mm(out, lhsT, rhs, start, stop):
        P.emit("pe", lambda e: e.matmul(out.ap, lhsT.ap, rhs.ap, start=start, stop=stop,
                                        skip_group_check=True),
               reads=regs(lhsT, rhs), writes=regs(out))

    def tr(out, in_):
        P.emit("pe", lambda e: e.matmul(out.ap, in_.ap, identb[:, :].ap, start=True, stop=True,
                                        skip_group_check=True),
               reads=regs(in_, identb[:, :]), writes=regs(out))

    def act(out, in_, func, bias=0.0, scale=1.0, eng="act"):
        P.emit(eng, lambda e: e.activation(out.ap, in_.ap, func, bias=apof(bias), scale=apof(scale)),
               reads=regs(in_, bias, scale), writes=regs(out))

    def ts(out, in0, s1, s2, op0, op1=None, eng="dve"):
        if op1 is None:
            P.emit(eng, lambda e: e.tensor_scalar(out.ap, in0.ap, apof(s1), None, op0),
                   reads=regs(in0, s1), writes=regs(out))
        else:
            P.emit(eng, lambda e: e.tensor_scalar(out.ap, in0.ap, apof(s1), apof(s2), op0, op1),
                   reads=regs(in0, s1, s2), writes=regs(out))

    def tt(out, in0, in1, op, eng="dve"):
        P.emit(eng, lambda e: e.tensor_tensor(out.ap, in0.ap, in1.ap, op),
               reads=regs(in0, in1), writes=regs(out))

    def stt(out, in0, scalar, in1, op0, op1, eng="dve"):
        P.emit(eng, lambda e: e.scalar_tensor_tensor(out.ap, in0.ap, apof(scalar), in1.ap, op0, op1),
               reads=regs(in0, scalar, in1), writes=regs(out))

    def cp(out, in_, eng="dve"):
        if eng == "act":
            P.emit(eng, lambda e: e.copy(out.ap, in_.ap), reads=regs(in_), writes=regs(out))
        else:
            P.emit(eng, lambda e: e.tensor_copy(out.ap, in_.ap), reads=regs(in_), writes=regs(out))

    def memset(out, val, eng="dve"):
        P.emit(eng, lambda e: e.memset(out.ap, val), writes=regs(out))

    def recip(out, in_):
        P.emit("dve", lambda e: e.reciprocal(out.ap, in_.ap), reads=regs(in_), writes=regs(out))

    def rsum(out, in_):
        P.emit("dve", lambda e: e.reduce_sum(out.ap, in_.ap, AX.X), reads=regs(in_), writes=regs(out))

    def bnstats(out, in_):
        P.emit("dve", lambda e: e.bn_stats(out.ap, in_.ap), reads=regs(in_), writes=regs(out))

    def bnaggr(out, in_):
        P.emit("dve", lambda e: e.bn_aggr(out.ap, in_.ap), reads=regs(in_), writes=regs(out))

    auto_slots = {}

    def dma(queue, out, in_, slot, final=False):
        sbside = out if isinstance(out, Opnd) else in_
        key = (queue, sbside.reg[1])
        if key not in auto_slots:
            auto_slots[key] = new_slot()
        slot = auto_slots[key]
        o = P.emit(queue, lambda e: e.dma_start(apof(out), apof(in_)),
                   reads=regs(in_), writes=regs(out), dma_slot=slot)
        if final:
            P.final_dma.append(o)
        return o

    def ln_stats(src, st6, mv, rs, nb):
        bnstats(st6[:, 0, :], Opnd(src.ap[:, 0:512], src.reg))
        bnstats(st6[:, 1, :], Opnd(src.ap[:, 512:1024], src.reg))
        bnaggr(mv[:, :], st6[:, :, :])
        act(rs, mv[:, 1:2], AF.Ln, bias=LN_EPS)
        act(rs, rs, AF.Exp, scale=-0.5)
        stt(nb, mv[:, 0:1], -1.0, rs, ALU.mult, ALU.mult)

    dma("sp", ident[:, :], ident_d, None)
    dma("sp", cT.flat(), cT_d, None)
    dma("sp", badaT[:, :], badaT_d, None)
    dma("sp", pscT[:, :], pscT_d, None)
    memset(ones[:, :], 1.0)
    cp(identb[:, :], ident[:, :])
    act(cactT[:, :, :], cT[:, :, :], AF.Silu)
    cp(CACTB[:, :, :], cactT[:, :, :])
    PIECE0 = [sb(OFF_HT, (8, 1024), BF16), sb(OFF_HT + 16 * K, (8, 1024), BF16)]
    pcount = 0
    bank_rr = [0]

    def next_bank():
        b = bank_rr[0]
        bank_rr[0] = (b + 1) % 8
        return b

    for k in (0, 1, 3, 4):
        pb = PIECE0[pcount % 2]
        dma("pool", pb.flat(), wada_d[k], None)
        pcount += 1
        for nt in range(8):
            bk = pbank(next_bank(), (512,))
            for c in range(8):
                mm(bk[:, 0:NSEQ], pb[:, c, nt * 128:(nt + 1) * 128], CACTB[:, c, :], c == 0, c == 7)
            i = k * 8 + nt
            if k in (1, 4):
                ts(modT[:, i, :], bk[:, 0:NSEQ], badaT[:, i:i + 1], 1.0, ALU.add, ALU.add)
            else:
                ts(modT[:, i, :], bk[:, 0:NSEQ], badaT[:, i:i + 1], None, ALU.add)

    def load_head_consts():
        dma("sp", biast.flat(), biast_d, None)
        dma("sp", fact.flat(), fact_d, None)
        dma("sp", distt.flat(), dist_d, None)
        dma("pool", amat.flat(), amat_d, None)
        dma("sp", subg[:, :], subg_d, None)
        dma("pool", wpool.flat(), wpool_d, None)
        ts(subg[:, :], subg[:, :], (1.0 - LAMBDA_INIT) * (128.0 ** 0.5), None, ALU.mult)

    def init_va():
        for b in range(2):
            memset(VA[b][:, :, 128:144], 1.0)
            memset(QTZ3[b][64:128, :, 0, :], 0.0)
            memset(QTZ3[b][0:64, :, 1, :], 0.0, eng="pool")

    def compute_lambda():
        dma("sp", lamv.flat(), lamv_d, None)
        tt(lamt[:, 0, :], lamv[:, 0, :], lamv[:, 1, :], ALU.mult)
        tt(lamt[:, 1, :], lamv[:, 2, :], lamv[:, 3, :], ALU.mult)
        rsum(lams[:, 0:1], lamt[:, 0, :])
        rsum(lams[:, 1:2], lamt[:, 1, :])
        act(lams[:, :], lams[:, :], AF.Exp)
        stt(neglam[:, :], lams[:, 1:2], -LAMBDA_INIT, lams[:, 0:1], ALU.add, ALU.subtract)

    ASTOP = 0

    def dbg_out(src):
        dma("sp", out_d[0:128, :], src, None, final=True)

    def phase_a0(s):
        r0 = s * T
        dma("pool", WU.flat(), wu_d, None)
        nbuf = 3
        for tt_ in range(min(2, NT)):
            dma("sp", XT[tt_ % nbuf][:, :], x_d[r0 + tt_ * 128: r0 + (tt_ + 1) * 128, :], None)
        for ti in range(NT):
            if ti + 2 < NT:
                dma("sp", XT[(ti + 2) % nbuf][:, :],
                    x_d[r0 + (ti + 2) * 128: r0 + (ti + 3) * 128, :], None)
            xt = XT[ti % nbuf]
            b = ti % 2
            ln_stats(xt[:, :], ST6[b], MV[b], RS[b][:, :], NB[b][:, :])
            if ASTOP == 1:
                cp(DBGT[:, 0:1], RS[b][:, :])
                cp(DBGT[:, 1:2], NB[b][:, :])
                cp(DBGT[:, 2:4], MV[b][:, :])
                dbg_out(DBGT[:, :])
                return
            act(XN[b][:, :], xt[:, :], AF.Identity, bias=NB[b][:, :], scale=RS[b][:, :])
            if ASTOP == 2:
                cp(DBGT[:, :], XN[b][:, :])
                dbg_out(DBGT[:, :])
                return
            for half in range(2):
                bk = pbank(next_bank(), (4, 128))
                for cc in range(4):
                    c = half * 4 + cc
                    if not None:
                        tr(bk[:, cc, :], XN[b][:, c * 128:(c + 1) * 128])
                for cc in range(4):
                    c = half * 4 + cc
                    if None:
                        continue
                    e = "act"
                    if None:
                        cp(hT[:, c, ti * 128:(ti + 1) * 128], bk[:, cc, :], eng=e)
                    elif e == "act":
                        act(hT[:, c, ti * 128:(ti + 1) * 128], bk[:, cc, :], AF.Identity,
                            bias=modT[:, 0 * 8 + c, s:s + 1], scale=modT[:, 1 * 8 + c, s:s + 1])
                    else:
                        ts(hT[:, c, ti * 128:(ti + 1) * 128], bk[:, cc, :],
                           modT[:, 1 * 8 + c, s:s + 1], modT[:, 0 * 8 + c, s:s + 1], ALU.mult, ALU.add)
            if ASTOP == 3:
                memset(DBGT[:, :], 0.0)
                if not None:
                    cp(DBGT[:, 0:128], hT[:, 0, 0:128])
                    cp(DBGT[:, 128:256], hT[:, 7, 0:128])
                dbg_out(DBGT[:, :])
                return
            bk = pbank(next_bank(), (512,))
            for c in range(8):
                mm(bk[:, :], hT[:, c, ti * 128:(ti + 1) * 128], WU[:, c, :], c == 0, c == 7)
            cp(UTOK[:, ti, :], bk[:, :], eng="act" if ti % 2 else "dve")
        for ti in range(NT):
            bk = pbank(next_bank(), (4, 128))
            for g in range(4):
                bands = []
                if ti > 0:
                    bands.append((ti - 1, 3))
                bands.append((ti, 0 if ti == 0 else (2 if ti == NT - 1 else 1)))
                if ti < NT - 1:
                    bands.append((ti + 1, 4))
                for bi, (tsrc, var) in enumerate(bands):
                    mm(bk[:, g, :], UTOK[:, tsrc, g * 128:(g + 1) * 128], amat[:, g, var, :],
                       bi == 0, bi == len(bands) - 1)
            cp(pooledT[:, :, ti * 128:(ti + 1) * 128], bk[:, :, :], eng="act" if ti % 2 else "dve")

    MISC = (6, 7)
    misc_rr = [0]

    def next_misc():
        b = MISC[misc_rr[0] % 2]
        misc_rr[0] += 1
        return b

    def head_proj(s, j, hb):
        w = WH[hb]
        g = j // 2
        for ch in range(NC5):
            tok = slice(ch * 512, (ch + 1) * 512)
            bk = pbank(next_misc(), (2, 256))
            for c in range(8):
                mm(bk.flat(), w[:, c, 0:128], hT[:, c, tok], c == 0, c == 7)
            cp(QTZ3[hb][0:64, 2 * ch:2 * ch + 2, 0, :], bk[0:64, :, :], eng="dve")
            cp(QTZ3[hb][64:128, 2 * ch:2 * ch + 2, 1, :], bk[64:128, :, :], eng="dve")
            bk = pbank(next_misc())
            for c in range(8):
                mm(bk[:, :], w[:, c, 128:256], hT[:, c, tok], c == 0, c == 7)
            cp(KT[hb][:, tok], bk[:, :], eng="act")
            bk = pbank(next_misc(), (4, 128))
            for t4 in range(4):
                ti = ch * 4 + t4
                for c in range(8):
                    mm(bk[:, t4, :], hT[:, c, ti * 128:(ti + 1) * 128], w[:, c, 256:384], c == 0, c == 7)
            cp(VA[hb][:, ch * 4:(ch + 1) * 4, 0:128], bk[:, :, :], eng="dve")
            bk = pbank(next_misc())
            for c in range(8):
                mm(bk[:, :], w[:, c, 384:512], hT[:, c, tok], c == 0, c == 7)
            act(SGA[hb][:, tok], bk[:, :], AF.Sigmoid)
            bk = pbank(next_misc())
            for c in range(8):
                mm(bk[:, :], w[:, c, 512:640], hT[:, c, tok], c == 0, c == 7)
            sg = SGP[0]
            act(sg[:, :], bk[:, :], AF.Sigmoid)
            bk = pbank(next_misc())
            mm(bk[:, :], wpool[:, g, (j % 2) * 128:(j % 2) * 128 + 128], pooledT[:, g, tok], True, True)
            stt(PP[hb][:, tok], bk[:, :], pscT[:, j:j + 1], sg[:, :], ALU.mult, ALU.mult)

    pt_rr = [0]
    sd_rr = [0]
    ep_rr = [0]
    aT_rr = [0]

    ATSTOP = 0

    pending = []

    def run_pending(force=False):
        items = pending[:]
        del pending[:]
        for item in items:
            item[0] -= 1
            if item[0] <= 0 or force:
                item[1]()
            else:
                pending.append(item)

    def flush_pending():
        guard = 0
        while pending and guard < 64:
            run_pending(force=True)
            guard += 1

    def attention(s, j, hb, mid_hook):
        slope = SLOPES[j]
        steps = [(c, kt) for c in range(NCH) for kt in range(NT)]
        nsteps_small = NT < 12

        def emit_qk(i):
            c, kt = steps[i]
            sbk = pbank(i % 3, (512,))
            ksl = slice(kt * 128, (kt + 1) * 128)
            mm(sbk[:, :], KT[hb][:, ksl], QTZ[hb][:, c * 512:(c + 1) * 512], True, True)

        def acc_views():
            base = 3 * 512
            A = [Buf("ps", psum_h[:, base: base + 320], base * 4, (2, 160), 4),
                 Buf("ps", psum_h[:, base + 672: base + 992], (base + 672) * 4, (2, 160), 4)]
            Bv = [Buf("ps", psum_h[:, base + 320: base + 704], (base + 320) * 4, (2, 192), 4),
                  Buf("ps", psum_h[:, base + 1024: base + 1344], (base + 1024) * 4, (2, 160), 4)]
            return A, Bv

        def epilogue(c, t, accs, e):
            hasL = c > 0
            hasR = c < NCH - 1
            Ar, Dr = accs[0][t][:, :, 0:129], accs[1][t][:, :, 0:129]
            FRf = fact[:, j, 2 * t + 1:2 * t + 2]
            ta, tb, tc = TMPA[0], TMPB[t], TMPC[e]
            tal = TAL[t]
            qt = slice((2 * c + t) * 128, (2 * c + t + 1) * 128)

            def s1a():
                if hasL and hasR:
                    stt(tb[:, :, :], Ar, FRf, tal[:, :, :], ALU.mult, ALU.add)

            def s1():
                if hasL and hasR:
                    tt(tc[:, :, :], Dr, tb[:, :, :], ALU.add)
                elif hasL:
                    tt(tc[:, :, :], Dr, tal[:, :, :], ALU.add)
                elif hasR:
                    act(ta[:, :, :], Dr, AF.Identity)
                    stt(tc[:, :, :], Ar, FRf, ta[:, :, :], ALU.mult, ALU.add)
                else:
                    act(tc[:, :, :], Dr, AF.Identity)

            def s2():
                recip(RDEN[e][:, :], tc[:, :, 128])
                ts(NL[e][:, :], RDEN[e][:, 1:2], neglam[:, :], None, ALU.mult)
                ts(O0[e][:, :], tc[:, 0, 0:128], RDEN[e][:, 0:1], None, ALU.mult)
                stt(OO[e][:, :], tc[:, 1, 0:128], NL[e][:, :], O0[e][:, :], ALU.mult, ALU.add)
                tt(SQ[e][:, :], OO[e][:, :], OO[e][:, :], ALU.mult)
                rsum(SS[e][:, :], SQ[e][:, :])

            def s3():
                act(RSTD[e][:, :], SS[e][:, :], AF.Ln, bias=128.0 * RMS_EPS)
                act(RSTD[e][:, :], RSTD[e][:, :], AF.Exp, scale=-0.5)

            def s4():
                stt(AOUT[e][:, :], OO[e][:, :], RSTD[e][:, :], subg[:, :], ALU.mult, ALU.mult)
                slot = aT_rr[0] % 8
                aT_rr[0] += 1
                bk = pbank(MISC[slot // 4], (4, 128))
                tr(bk[:, slot % 4, :], AOUT[e][:, :])

                def s5(bk=bk, slot=slot):
                    tt(GT1[e][:, :], bk[:, slot % 4, :], SGA[hb][:, qt], ALU.mult)
                    tt(mixedT[:, j, qt], GT1[e][:, :], PP[hb][:, qt], ALU.add)
                pending.append([2, s5])

            return s1a, s1, s2, s3, s4

        emit_qk(0)
        if len(steps) > 1:
            emit_qk(1)
        accs = acc_views()
        for i, (c, kt) in enumerate(steps):
            if kt == 0:
                if c == NCH // 2 and mid_hook is not None:
                    mid_hook()
            if i + 2 < len(steps):
                emit_qk(i + 2)
            sbk = pbank(i % 3, (2, 256))
            pt = PT[pt_rr[0] % 3]
            pt_rr[0] += 1
            if kt < 2 * c:
                cat = 0
                bidx = 2 * c - kt
                act(pt.flat(), sbk.flat(), AF.Exp, bias=biast[:, j, bidx:bidx + 1], scale=QK_SCALE)
            elif kt > 2 * c + 1:
                cat = 2
                bidx = 16 + (kt - 2 * c - 1)
                act(pt.flat(), sbk.flat(), AF.Exp, bias=biast[:, j, bidx:bidx + 1], scale=QK_SCALE)
            else:
                cat = 1
                ii = kt - 2 * c
                sd = SD[sd_rr[0] % 2]
                sd_rr[0] += 1
                stt(sd[:, :, :], distt[:, ii, :, :], -slope / QK_SCALE, sbk[:, :, :], ALU.mult, ALU.add)
                act(pt.flat(), sd.flat(), AF.Exp, scale=QK_SCALE)
            if cat == 0:
                first, last = kt == 0, kt == 2 * c - 1
            elif cat == 1:
                first, last = kt == 2 * c, kt == 2 * c + 1
            else:
                first, last = kt == 2 * c + 2, kt == NT - 1
            seen = set()
            for t in range(2):
                dst = accs[1][t] if cat == 1 else accs[0][t]
                for m in range(2):
                    if cat == 1:
                        bank = 3 if (t == 0 and m == 0) else (4 if t == 0 else 5)
                    else:
                        bank = 3 if t == 0 else 4
                    st = first and (bank not in seen)
                    seen.add(bank)
                    mm(dst[:, m, 0:132], pt[:, m, t * 128:(t + 1) * 128], VA[hb][:, kt, 0:132], st, last)
            if cat == 0 and last:
                for t in range(2):
                    act(TAL[t][:, :, :], accs[0][t][:, :, 0:129], AF.Identity, scale=fact[:, j, 2 * t:2 * t + 1])
            run_pending()
            if kt == NT - 1:
                if NT < 12:
                    flush_pending()
                st = []
                for t in range(2):
                    e = ep_rr[0] % 2
                    ep_rr[0] += 1
                    st.append(epilogue(c, t, accs, e))
                st[0][0]()
                st[1][0]()
                st[0][1]()
                st[1][1]()
                for t in range(2):
                    pending.append([1 + t, st[t][2]])
                    pending.append([3 + t, st[t][3]])
                    pending.append([5 + t, st[t][4]])
        if nsteps_small:
            flush_pending()
        return False

    def bcast_mod_tile(s, k, piece, gtile, bidx):
        dma("pool", piece.flat(), wada_d[k], None)
        dma("sp", BADAG[:, :], badag_d[:, bidx * 1024:(bidx + 1) * 1024], None)
        for c in range(8):
            act(CBT[:, c, :], ones[:, :], AF.Identity, scale=cactT[:, c, s:s + 1])
        for half in range(2):
            bk = pbank(next_bank())
            for c in range(8):
                mm(bk[:, :], CBT[:, c, :], piece[:, c, half * 512:(half + 1) * 512], c == 0, c == 7)
            tt(gtile[:, half * 512:(half + 1) * 512], bk[:, :], BADAG[:, half * 512:(half + 1) * 512], ALU.add)

    def phase_c(s):
        r0 = s * T
        dma("pool", WOUT.flat(), wout_d, None)
        dma("sp", LN1.flat(), ln_d[:, 0:2048], None)
        bcast_mod_tile(s, 2, PIECE_C, G1T, 0)
        dma("sp", CX[0][:, :], x_d[r0: r0 + 128, :], None)
        for ti in range(NT):
            b = ti % 2
            if ti + 1 < NT:
                dma("sp", CX[1 - b][:, :], x_d[r0 + (ti + 1) * 128: r0 + (ti + 2) * 128, :], None)
            r = CR[b]
            for half in range(2):
                bk = pbank(next_bank())
                hs = slice(half * 512, (half + 1) * 512)
                for c in range(8):
                    mm(bk[:, :], mixedT[:, c, ti * 128:(ti + 1) * 128], WOUT[:, c, hs], c == 0, c == 7)
                tt(r[:, hs], bk[:, :], G1T[:, hs], ALU.mult)
            stt(r[:, :], CX[b][:, :], ALPHA, r[:, :], ALU.mult, ALU.add)
            ln_stats(r[:, :], CST6[b], CMV[b], CRS[b][:, :], CNB[b][:, :])
            act(r[:, :], r[:, :], AF.Identity, bias=CNB[b][:, :], scale=CRS[b][:, :])
            tt(r[:, :], r[:, :], LN1[:, 0, :], ALU.mult)
            tt(x1[:, ti, :], r[:, :], LN1[:, 1, :], ALU.add, eng="pool")
            ln_stats(x1[:, ti, :], CST6b[b], CMVb[b], RS2[:, ti:ti + 1], NB2[:, ti:ti + 1])

    def phase_ffn(s):
        r0 = s * T
        dma("pool", WFO.flat(), wfo_d, None)
        dma("sp", LN2.flat(), ln_d[:, 2048:4096], None)
        bcast_mod_tile(s, 5, PIECE_F, G2T, 1)
        nw = 3
        wcount = [0]

        def issue_wfi(ft):
            k_ = wcount[0] % nw
            wcount[0] += 1
            dma("pool", WFI[k_].flat(), wfi_d[ft], None)
            return k_

        for ch in range(NC5):
            h2 = H2T[0]
            pend = [issue_wfi(0), issue_wfi(1)]
            for t4 in range(4):
                ti = ch * 4 + t4
                b = ti % 2
                act(FXN[b][:, :], x1[:, ti, :], AF.Identity, bias=NB2[:, ti:ti + 1], scale=RS2[:, ti:ti + 1])
                for half in range(2):
                    bk = pbank(next_bank(), (4, 128))
                    for cc in range(4):
                        c = half * 4 + cc
                        tr(bk[:, cc, :], FXN[b][:, c * 128:(c + 1) * 128])
                    for cc in range(4):
                        c = half * 4 + cc
                        if True:
                            act(h2[:, c, t4 * 128:(t4 + 1) * 128], bk[:, cc, :], AF.Identity,
                                bias=modT[:, 3 * 8 + c, s:s + 1], scale=modT[:, 4 * 8 + c, s:s + 1])
                        else:
                            ts(h2[:, c, t4 * 128:(t4 + 1) * 128], bk[:, cc, :],
                               modT[:, 4 * 8 + c, s:s + 1], modT[:, 3 * 8 + c, s:s + 1], ALU.mult, ALU.add)
            for ft in range(NFT):
                wk = WFI[pend.pop(0)]
                if ft + 2 < NFT:
                    pend.append(issue_wfi(ft + 2))
                bg = pbank(next_bank())
                for c in range(8):
                    mm(bg[:, :], wk[:, c, 0:128], h2[:, c, :], c == 0, c == 7)
                bu = pbank(next_bank())
                for c in range(8):
                    mm(bu[:, :], wk[:, c, 128:256], h2[:, c, :], c == 0, c == 7)
                sg = SG[ft % 2]
                act(sg[:, :], bg[:, :], AF.Silu)
                tt(ACTT[:, ft, :], bu[:, :], sg[:, :], ALU.mult)
            for t4 in range(4):
                ti = ch * 4 + t4
                b = ti % 2
                r = FR_[b]
                for half in range(2):
                    bk = pbank(next_bank())
                    hs = slice(half * 512, (half + 1) * 512)
                    for ft in range(NFT):
                        mm(bk[:, :], ACTT[:, ft, t4 * 128:(t4 + 1) * 128], WFO[:, ft, hs], ft == 0, ft == NFT - 1)
                    tt(r[:, hs], bk[:, :], G2T[:, hs], ALU.mult)
                stt(r[:, :], x1[:, ti, :], ALPHA, r[:, :], ALU.mult, ALU.add)
                ln_stats(r[:, :], FST6[b], FMV[b], FRS[b][:, :], FNB[b][:, :])
                act(r[:, :], r[:, :], AF.Identity, bias=FNB[b][:, :], scale=FRS[b][:, :])
                tt(r[:, :], r[:, :], LN2[:, 0, :], ALU.mult)
                tt(r[:, :], r[:, :], LN2[:, 1, :], ALU.add, eng="pool")
                dma("sp", out_d[r0 + ti * 128: r0 + (ti + 1) * 128, :], r[:, :], None, final=True)

    compute_lambda_done = [False]
    KSTOP = 0

    def dbg_out(src):
        dma("sp", out_d[0:128, :], src, None, final=True)

    MICRO = 0
    for s in range(NSEQ if not MICRO else 0):
        if KSTOP == 1:
            cp(DBGT[:, 0:48 * NSEQ], Opnd(modT.ap2d, modT.flat().reg))
            dbg_out(DBGT[:, :])
            break
        load_head_consts()
        if not compute_lambda_done[0]:
            compute_lambda()
            compute_lambda_done[0] = True
        if KSTOP == 2:
            cp(DBGT[:, 0:1], neglam[:, :])
            cp(DBGT[:, 128:256], subg[:, :])
            dbg_out(DBGT[:, :])
            break
        phase_a0(s)
        if ASTOP:
            break
        init_va()
        if KSTOP == 3:
            cp(DBGT[:, 0:512], pooledT[:, 0, 0:512])
            cp(DBGT[:, 512:1024], hT[:, 0, 0:512])
            dbg_out(DBGT[:, :])
            break
        dma("pool", WH[0].flat(), wh_d[0], None)
        head_proj(s, 0, 0)
        if KSTOP == 4:
            cp(DBGT[:, 0:512], QTZ[0][:, 0:512])
            cp(DBGT[:, 512:1024], PP[0][:, 0:512])
            dbg_out(DBGT[:, :])
            break
        for j in range(NH):
            hb = j % 2
            hook = None
            if j + 1 < NH:
                dma("pool", WH[1 - hb].flat(), wh_d[j + 1], None)
                hook = (lambda jj=j + 1, bb=1 - hb: head_proj(s, jj, bb))
            if attention(s, j, hb, hook):
                break
            if hook is not None and NCH // 2 == 0:
                hook()
            if KSTOP == 5:
                break
        if ATSTOP:
            break
        if KSTOP in (5, 6):
            cp(DBGT[:, 0:512], mixedT[:, 0, 0:512])
            cp(DBGT[:, 512:1024], mixedT[:, 7, 0:512])
            dbg_out(DBGT[:, :])
            break
        flush_pending()
        phase_c(s)
        if KSTOP == 7:
            dbg_out(x1[:, 0, :])
            break
        phase_ffn(s)

    sem_cms = {e: nc.semaphore("s_" + e) for e in ("pe", "act", "dve", "pool")}
    sems = {e: cm.__enter__() for e, cm in sem_cms.items()}
    for e in ("pe", "act", "dve", "pool"):
        P.replay(e, None, sems)
    with nc.Block() as block:
        @block.tensor
        def _(eng):
            P.run_engine("pe", eng, sems)

        @block.scalar
        def _(eng):
            P.run_engine("act", eng, sems)

        @block.vector
        def _(eng):
            P.run_engine("dve", eng, sems)

        @block.gpsimd
        def _(eng):
            P.run_engine("pool", eng, sems)

        @block.sync
        def _(eng):
            P.run_engine("sp", eng, sems)
    return nc, P


def _chunk_rows(w):
    k, n = w.shape
    return np.ascontiguousarray(w.reshape(k // 128, 128, n).transpose(1, 0, 2))


def prep_shared(inp):
    f = lambda a: np.ascontiguousarray(np.asarray(a, dtype=np.float32))
    w_ada = f(inp["w_ada"])[0]
    b_ada = f(inp["b_ada"])[0]
    w_in = f(inp["w_in"])[0]
    sh = {}
    sh["wada"] = np.ascontiguousarray(
        np.stack([_chunk_rows(w_ada[:, k * 1024:(k + 1) * 1024]).reshape(128, 8 * 1024) for k in range(6)]))
    sh["badaT"] = np.ascontiguousarray(b_ada.reshape(48, 128).T)
    sh["badag"] = np.ascontiguousarray(
        np.broadcast_to(np.concatenate([b_ada[2048:3072], b_ada[5120:6144]])[None, :], (128, 2048)))
    sh["wu"] = _chunk_rows(w_in[:, 3072:3584]).reshape(128, 8 * 512)
    wh = []
    for j in range(NH):
        cols = np.concatenate([
            w_in[:, j * 128:(j + 1) * 128],
            w_in[:, 1024 + j * 128:1024 + (j + 1) * 128],
            w_in[:, 2048 + j * 128:2048 + (j + 1) * 128],
            w_in[:, 3584 + j * 128:3584 + (j + 1) * 128],
            w_in[:, 4608 + j * 128:4608 + (j + 1) * 128]], axis=1)
        wh.append(_chunk_rows(cols).reshape(128, 8 * 640))
    sh["wh"] = np.ascontiguousarray(np.stack(wh))
    sh["wpool"] = np.ascontiguousarray(f(inp["w_pool"])[0].transpose(1, 0, 2)).reshape(128, 4 * 256)
    sh["wout"] = _chunk_rows(f(inp["w_out"])[0]).reshape(128, 8 * 1024)
    wfi_full = f(inp["w_ffn_in"])[0]
    wfi = []
    for ft in range(NFT):
        cols = np.concatenate([wfi_full[:, ft * 128:(ft + 1) * 128],
                               wfi_full[:, DFF + ft * 128:DFF + (ft + 1) * 128]], axis=1)
        wfi.append(_chunk_rows(cols).reshape(128, 8 * 256))
    sh["wfi"] = np.ascontiguousarray(np.stack(wfi))
    sh["wfo"] = _chunk_rows(f(inp["w_ffn_out"])[0]).reshape(128, NFT * 1024)
    lam = np.stack([f(inp["lambda_q1"])[0], f(inp["lambda_k1"])[0],
                    f(inp["lambda_q2"])[0], f(inp["lambda_k2"])[0]]).reshape(1, 256)
    sh["lamv"] = np.ascontiguousarray(np.broadcast_to(lam, (128, 256)))
    sh["subg"] = np.ascontiguousarray(np.broadcast_to(f(inp["sub_g"])[0][None, :], (128, 128)))
    sh["pscT"] = np.ascontiguousarray(f(inp["pool_scale"])[0].reshape(8, 128).T)
    lnp = np.concatenate([f(inp["ln1_g"])[0], f(inp["ln1_b"])[0], f(inp["ln2_g"])[0], f(inp["ln2_b"])[0]])
    sh["lnp"] = np.ascontiguousarray(np.broadcast_to(lnp[None, :], (128, 4096)))
    ident, bias, fac, dist, amat = _make_tables()
    sh["ident"] = ident
    sh["biast"] = bias.reshape(128, NH * 32)
    sh["fact"] = fac.reshape(128, NH * 4)
    sh["dist"] = dist.reshape(128, 1024)
    sh["amat"] = amat.reshape(128, 4 * 5 * 128)
    return sh


def prep_core(x, c, core, T, NSEQ):
    xs = np.ascontiguousarray(x[core * NSEQ:(core + 1) * NSEQ].reshape(NSEQ * T, D), dtype=np.float32)
    cs = np.asarray(c[core * NSEQ:(core + 1) * NSEQ], dtype=np.float32)
    cT = np.ascontiguousarray(cs.T.reshape(8, 128, NSEQ).transpose(1, 0, 2)).reshape(128, 8 * NSEQ)
    return {"x": xs, "cT": cT}


def kernel(**inputs):
    x = np.asarray(inputs["x"], dtype=np.float32)
    c = np.asarray(inputs["c"], dtype=np.float32)
    B, T, _ = x.shape
    NSEQ = B // N_CORES
    nc, _ = build(T, NSEQ)
    sh = prep_shared(inputs)
    in_maps = []
    for core in range(N_CORES):
        m = dict(sh)
        m.update(prep_core(x, c, core, T, NSEQ))
        in_maps.append(m)
    res = run_bass_kernel_spmd(nc, in_maps, core_ids=list(range(N_CORES)))
    outs = [np.asarray(r["out"], dtype=np.float32).reshape(NSEQ, T, D) for r in res.results]
    return np.concatenate(outs, axis=0)
```

```python
import numpy as np
import concourse.bass as bass
import concourse.mybir as mybir
from concourse.bass_utils import run_bass_kernel_spmd

F32 = mybir.dt.float32
BF16 = mybir.dt.bfloat16
AF = mybir.ActivationFunctionType
ALU = mybir.AluOpType
AX = mybir.AxisListType

D = 1024
NH = 8
DFF = 2816
NFT = DFF // 128
POOL_W = (2, 4, 8, 16)
ALPHA = 2.0 ** 0.25
LN_EPS = 1e-5
RMS_EPS = 1e-5
LAMBDA_INIT = 0.8 - 0.6 * 1.0
QK_SCALE = 0.125
SLOPES = [2.0 ** (-8.0 * (h + 1) / NH) for h in range(NH)]
N_CORES = 8

GR = 512


class Opnd:
    __slots__ = ("ap", "reg")

    def __init__(self, ap, reg):
        self.ap = ap
        self.reg = reg


class Buf:
    def __init__(self, space, ap2d, off_bytes, dims, esz):
        self.space = space
        self.off = off_bytes
        self.dims = tuple(dims)
        self.esz = esz
        n = 1
        strides = []
        for d in reversed(self.dims):
            strides.append(n)
            n *= d
        self.strides = tuple(reversed(strides))
        self.n = n
        self.ap2d = ap2d
        if len(self.dims) == 1:
            self.ap = ap2d
        elif len(self.dims) == 2:
            self.ap = ap2d.rearrange("p (a b) -> p a b", b=self.dims[1])
        elif len(self.dims) == 3:
            self.ap = ap2d.rearrange("p (a b c) -> p a b c", b=self.dims[1], c=self.dims[2])
        else:
            raise ValueError

    def flat(self):
        return Opnd(self.ap2d, (self.space, self.off, self.off + self.n * self.esz))

    def __getitem__(self, idx):
        if not isinstance(idx, tuple):
            idx = (idx,)
        idx = idx + (slice(None),) * (1 + len(self.dims) - len(idx))
        lo = 0
        hi = 0
        for i, d in enumerate(self.dims):
            s = idx[i + 1]
            if isinstance(s, int):
                a, b = s, s + 1
            else:
                a = 0 if s.start is None else s.start
                b = d if s.stop is None else s.stop
            assert 0 <= a < b <= d, (a, b, d)
            lo += a * self.strides[i]
            hi += (b - 1) * self.strides[i]
        hi += 1
        return Opnd(self.ap[idx], (self.space, self.off + lo * self.esz, self.off + hi * self.esz))


class Op:
    __slots__ = ("eng", "fn", "deps", "idx", "needs_inc", "count", "dma", "sem", "semcnt")


class DmaSlot:
    def __init__(self, sem):
        self.sem = sem
        self.cnt = 0


class Prog:
    ENGS = ("pe", "act", "dve", "pool", "sp")

    def __init__(self, nc):
        self.nc = nc
        self.ops = {e: [] for e in self.ENGS}
        self.lastw = {}
        self.readers = {}
        self.waited = {e: {} for e in self.ENGS}
        self.final_dma = []
        self.all_slots = []

    def _grans(self, reg):
        sp, lo, hi = reg
        gr = 2048 if sp == "ps" else GR
        return [(sp, g) for g in range(lo // gr, (hi - 1) // gr + 1)]

    def emit(self, eng, fn, reads=(), writes=(), dma_slot=None):
        op = Op()
        op.eng = eng
        op.fn = fn
        op.idx = len(self.ops[eng])
        op.needs_inc = False
        op.count = None
        op.dma = dma_slot is not None
        if op.dma:
            dma_slot.cnt += 16
            op.sem = dma_slot.sem
            op.semcnt = dma_slot.cnt
        deps = {}
        for reg in reads:
            if reg is None:
                continue
            for g in self._grans(reg):
                w = self.lastw.get(g)
                if w is not None:
                    deps[id(w)] = w
                if reg[0] == "ps":
                    rd = self.readers.get(g)
                    if rd:
                        for e2, r in rd[0].items():
                            if e2 != eng:
                                deps[id(r)] = r
        for reg in writes:
            if reg is None:
                continue
            for g in self._grans(reg):
                w = self.lastw.get(g)
                if w is not None:
                    deps[id(w)] = w
                rd = self.readers.get(g)
                if rd:
                    for r in rd[0].values():
                        deps[id(r)] = r
                    for r in rd[1]:
                        deps[id(r)] = r
        final = []
        best = {}
        for d in deps.values():
            if d.dma:
                final.append(d)
                continue
            if d.eng == eng and eng == "pe" and not op.dma:
                continue
            if self.waited[eng].get(d.eng, -1) >= d.idx:
                continue
            b = best.get(d.eng)
            if b is None or d.idx > b.idx:
                best[d.eng] = d
        for d in best.values():
            d.needs_inc = True
            self.waited[eng][d.eng] = d.idx
            final.append(d)
        op.deps = final
        for reg in writes:
            if reg is None:
                continue
            for g in self._grans(reg):
                self.lastw[g] = op
                self.readers[g] = ({}, [])
        for reg in reads:
            if reg is None:
                continue
            for g in self._grans(reg):
                rd = self.readers.get(g)
                if rd is None:
                    rd = ({}, [])
                    self.readers[g] = rd
                if op.dma:
                    rd[1].append(op)
                else:
                    rd[0][eng] = op
        self.ops[eng].append(op)
        return op

    def replay(self, eng_name, eng, sems):
        dma_waited = {}
        own = sems[eng_name]
        cnt = 0
        for op in self.ops[eng_name]:
            if op.needs_inc and not op.dma:
                cnt += 1
                op.count = cnt
        return cnt

    def run_engine(self, eng_name, eng, sems):
        dma_waited = {}
        sem_waited = {}
        for op in self.ops[eng_name]:
            for d in op.deps:
                if d.dma:
                    key = d.sem.num
                    if dma_waited.get(key, 0) >= d.semcnt:
                        continue
                    dma_waited[key] = d.semcnt
                    eng.wait_ge(d.sem, d.semcnt)
                else:
                    s = sems[d.eng]
                    if sem_waited.get(d.eng, 0) >= d.count:
                        continue
                    sem_waited[d.eng] = d.count
                    eng.wait_ge(s, d.count)
            ins = op.fn(eng)
            if op.dma:
                ins.then_inc(op.sem, 16)
            elif op.needs_inc:
                ins.then_inc(sems[eng_name], 1)
        if eng_name == "sp":
            for slot in self.all_slots:
                if slot.cnt > 0:
                    eng.wait_ge(slot.sem, slot.cnt)


def _make_amat():
    A = np.zeros((128, 4, 5, 128), np.float32)
    for g, w in enumerate(POOL_W):
        half = w // 2
        for j in range(128):
            lo, hi = max(j - half, 0), j + half
            for p in range(lo, min(hi, 128)):
                A[p, g, 0, j] += 1.0 / (hi - lo)
            A[j, g, 0, j] -= 1.0
            lo, hi = j - half, j + half
            for p in range(max(lo, 0), min(hi, 128)):
                A[p, g, 1, j] += 1.0 / w
            A[j, g, 1, j] -= 1.0
            lo, hi = j - half, min(j + half, 128)
            for p in range(max(lo, 0), hi):
                A[p, g, 2, j] += 1.0 / (hi - lo)
            A[j, g, 2, j] -= 1.0
            for p in range(128):
                if p - 128 >= j - half:
                    A[p, g, 3, j] += 1.0 / w
            for p in range(128):
                if p + 128 < j + half:
                    A[p, g, 4, j] += 1.0 / w
    return A


def _make_tables():
    p = np.arange(128, dtype=np.float64)
    ident = np.eye(128, dtype=np.float32)
    bias = np.zeros((128, NH, 32), np.float64)
    fac = np.zeros((128, NH, 4), np.float64)
    for h in range(NH):
        s = SLOPES[h]
        for m in range(16):
            bias[:, h, m] = -s * (128.0 * m - p)
            bias[:, h, 16 + m] = -s * (128.0 * m - 127.0 + p)
        for t in range(2):
            fac[:, h, 2 * t + 0] = np.exp(-s * (t * 128.0 + p))
            fac[:, h, 2 * t + 1] = np.exp(-s * (255.0 - t * 128.0 - p))
    dist = np.zeros((128, 2, 2, 256), np.float32)
    jq = np.arange(256, dtype=np.float64)
    for i in range(2):
        dd = np.abs(jq[None, :] - (i * 128.0 + p[:, None]))
        dist[:, i, 0, :] = dd
        dist[:, i, 1, :] = dd
    return ident, bias.astype(np.float32), fac.astype(np.float32), dist, _make_amat()


def build(T, NSEQ):
    NT = T // 128
    NCH = T // 256
    NC5 = T // 512
    nc = bass.Bass("TRN2", target_bir_lowering=False)

    def din(name, shape):
        return nc.dram_tensor(name, list(shape), F32, kind="ExternalInput").ap()

    x_d = din("x", (NSEQ * T, D))
    cT_d = din("cT", (128, 8 * NSEQ))
    wada_d = din("wada", (6, 128, 8 * 1024))
    badaT_d = din("badaT", (128, 48))
    badag_d = din("badag", (128, 2 * 1024))
    wu_d = din("wu", (128, 8 * 512))
    wh_d = din("wh", (NH, 128, 8 * 640))
    wpool_d = din("wpool", (128, 4 * 256))
    wout_d = din("wout", (128, 8 * 1024))
    wfi_d = din("wfi", (NFT, 128, 8 * 256))
    wfo_d = din("wfo", (128, NFT * 1024))
    lamv_d = din("lamv", (128, 4 * 64))
    subg_d = din("subg", (128, 128))
    pscT_d = din("pscT", (128, 8))
    ln_d = din("lnp", (128, 4 * 1024))
    ident_d = din("ident", (128, 128))
    biast_d = din("biast", (128, NH * 32))
    fact_d = din("fact", (128, NH * 4))
    dist_d = din("dist", (128, 2 * 2 * 256))
    amat_d = din("amat", (128, 4 * 5 * 128))
    out_d = nc.dram_tensor("out", [NSEQ * T, D], F32, kind="ExternalOutput").ap()

    ARENA = 207 * 1024
    arena_cm = nc.sbuf_tensor("arena", [128, ARENA // 2], BF16)
    psum_cm = nc.psum_tensor("ps", [128, 4096], F32)
    arena_h = arena_cm.__enter__()
    psum_h = psum_cm.__enter__()
    arena_bf = arena_h
    arena_f32 = arena_h.bitcast(F32)

    def sb(off, dims, dt):
        n = int(np.prod(dims))
        assert off % 32 == 0
        if dt is F32:
            assert off + n * 4 <= ARENA, (off, n)
            return Buf("sb", arena_f32[:, off // 4: off // 4 + n], off, dims, 4)
        assert off + n * 2 <= ARENA, (off, n)
        return Buf("sb", arena_bf[:, off // 2: off // 2 + n], off, dims, 2)

    psum_bf = psum_h.bitcast(BF16)

    def pbank_bf(b, dims):
        n = int(np.prod(dims))
        assert n <= 1024
        return Buf("ps", psum_bf[:, b * 1024: b * 1024 + n], b * 2048, dims, 2)

    def pbank(b, dims=(512,), nb=1):
        n = int(np.prod(dims))
        assert n <= 512 * nb
        return Buf("ps", psum_h[:, b * 512: b * 512 + n], b * 2048, dims, 4)

    K = 1024

    class Alloc:
        def __init__(self, base, limit):
            self.p = base
            self.limit = limit

        def __call__(self, dims, dt, gran=True):
            n = int(np.prod(dims)) * (4 if dt is F32 else 2)
            off = self.p
            self.p += -(-n // GR) * GR
            assert self.p <= self.limit, ("arena overflow", self.p, self.limit)
            return sb(off, dims, dt)

    pa = Alloc(0, 6 * K)
    ident = pa((128,), F32)
    ones = pa((128,), F32)
    identb = pa((128,), BF16)
    modT = pa((48, NSEQ), F32)
    neglam = pa((1,), F32)
    pscT = pa((8,), F32)
    cT = pa((8, NSEQ), F32)
    cactT = pa((8, NSEQ), F32)
    badaT = pa((48,), F32)
    CACTB = pa((8, NSEQ), BF16)
    P_END = 6 * K

    OFF_MIXED = P_END
    OFF_HT = OFF_MIXED + 32 * K
    OFF_POOLED = OFF_HT + 32 * K
    OFF_H = OFF_POOLED + 16 * K
    mixedT = sb(OFF_MIXED, (8, T), BF16)
    hT = sb(OFF_HT, (8, T), BF16)
    pooledT = sb(OFF_POOLED, (4, T), BF16)
    x1 = sb(OFF_HT, (NT, 1024), F32)
    OFF_X1_END = OFF_HT + 64 * K

    ha = Alloc(OFF_H, ARENA)
    biast = ha((NH, 32), F32)
    fact = ha((NH, 4), F32)
    distt = ha((2, 2, 256), F32)
    amat = ha((4, 5, 128), BF16)
    subg = ha((128,), F32)
    wpool = ha((4, 256), BF16)
    lamv = ha((4, 64), F32)
    lamt = ha((2, 64), F32)
    lams = ha((2,), F32)
    QTZ = [ha((2 * T,), BF16) for _ in range(2)]
    QTZ3 = [Buf("sb", q.ap2d, q.off, (NCH, 2, 256), 2) for q in QTZ]
    TAL = [ha((2, 129), F32) for _ in range(2)]
    KT = [ha((T,), BF16) for _ in range(2)]
    VA = [ha((NT, 144), BF16) for _ in range(2)]
    SGA = [ha((T,), BF16) for _ in range(2)]
    PP = [ha((T,), F32) for _ in range(2)]
    WH = [ha((8, 640), BF16) for _ in range(2)]
    PT = [ha((2, 256), BF16) for _ in range(3)]
    SD = [ha((2, 256), F32) for _ in range(2)]
    SGP = [ha((512,), F32) for _ in range(1)]
    TMPA = [ha((2, 129), F32) for _ in range(1)]
    TMPB = [ha((2, 129), F32) for _ in range(2)]
    TMPC = [ha((2, 129), F32) for _ in range(2)]
    RDEN = [ha((2,), F32) for _ in range(2)]
    NL = [ha((1,), F32) for _ in range(2)]
    O0 = [ha((128,), F32) for _ in range(2)]
    OO = [ha((128,), F32) for _ in range(2)]
    SQ = [ha((128,), F32) for _ in range(2)]
    SS = [ha((1,), F32) for _ in range(2)]
    RSTD = [ha((1,), F32) for _ in range(2)]
    AOUT = [ha((128,), BF16) for _ in range(2)]
    GT1 = [ha((128,), F32) for _ in range(2)]
    H_END = ha.p

    a0 = Alloc(OFF_H + 24 * K, ARENA)
    XT = [a0((1024,), F32) for _ in range(3)]
    XN = [a0((1024,), BF16) for _ in range(2)]
    DBGT = a0((1024,), F32)
    WU = a0((8, 512), BF16)
    UTOK = a0((NT, 512), BF16)
    ST6 = [a0((2, 6), F32) for _ in range(2)]
    MV = [a0((2,), F32) for _ in range(2)]
    RS = [a0((1,), F32) for _ in range(2)]
    NB = [a0((1,), F32) for _ in range(2)]

    ka = Alloc(max(OFF_X1_END, OFF_H), ARENA)
    RS2 = ka((NT,), F32)
    NB2 = ka((NT,), F32)
    CBT = ka((8, 128), BF16)
    BADAG = ka((1024,), F32)
    KEEP_END = ka.p
    ca = Alloc(KEEP_END, ARENA)
    WOUT = ca((8, 1024), BF16)
    PIECE_C = ca((8, 1024), BF16)
    G1T = ca((1024,), F32)
    LN1 = ca((2, 1024), F32)
    CX = [ca((1024,), F32) for _ in range(2)]
    CR = [ca((1024,), F32) for _ in range(2)]
    CST6 = [ca((2, 6), F32) for _ in range(2)]
    CMV = [ca((2,), F32) for _ in range(2)]
    CRS = [ca((1,), F32) for _ in range(2)]
    CNB = [ca((1,), F32) for _ in range(2)]
    CST6b = [ca((2, 6), F32) for _ in range(2)]
    CMVb = [ca((2,), F32) for _ in range(2)]

    fa1 = Alloc(OFF_MIXED, OFF_HT)
    WFI = [fa1((8, 256), BF16) for _ in range(3)]
    H2T = [fa1((8, 512), BF16) for _ in range(1)]
    G2T = fa1((1024,), F32)
    SG = [fa1((512,), F32) for _ in range(2)]
    fa2 = Alloc(KEEP_END, ARENA)
    WFO = fa2((NFT, 1024), BF16)
    ACTT = fa2((NFT, 512), BF16)
    PIECE_F = sb(ACTT.off, (8, 1024), BF16)
    LN2 = fa2((2, 1024), F32)
    FXN = [fa2((1024,), BF16) for _ in range(2)]
    FR_ = [fa2((1024,), F32) for _ in range(2)]
    FST6 = [fa2((2, 6), F32) for _ in range(2)]
    FMV = [fa2((2,), F32) for _ in range(2)]
    FRS = [fa2((1,), F32) for _ in range(2)]
    FNB = [fa2((1,), F32) for _ in range(2)]

    P = Prog(nc)
    n_slots = [0]
    slot_cms = []

    def new_slot():
        cm = nc.semaphore("d%d" % n_slots[0])
        n_slots[0] += 1
        s = cm.__enter__()
        slot_cms.append(cm)
        sl = DmaSlot(s)
        P.all_slots.append(sl)
        return sl

    def regs(*os):
        return [o.reg for o in os if isinstance(o, Opnd)]

    def apof(o):
        return o.ap if isinstance(o, Opnd) else o

    def mm(out, lhsT, rhs, start, stop):
        P.emit("pe", lambda e: e.matmul(out.ap, lhsT.ap, rhs.ap, start=start, stop=stop,
                                        skip_group_check=True),
               reads=regs(lhsT, rhs), writes=regs(out))

    def tr(out, in_):
        P.emit("pe", lambda e: e.matmul(out.ap, in_.ap, identb[:, :].ap, start=True, stop=True,
                                        skip_group_check=True),
               reads=regs(in_, identb[:, :]), writes=regs(out))

    def act(out, in_, func, bias=0.0, scale=1.0, eng="act"):
        P.emit(eng, lambda e: e.activation(out.ap, in_.ap, func, bias=apof(bias), scale=apof(scale)),
               reads=regs(in_, bias, scale), writes=regs(out))

    def ts(out, in0, s1, s2, op0, op1=None, eng="dve"):
        if op1 is None:
            P.emit(eng, lambda e: e.tensor_scalar(out.ap, in0.ap, apof(s1), None, op0),
                   reads=regs(in0, s1), writes=regs(out))
        else:
            P.emit(eng, lambda e: e.tensor_scalar(out.ap, in0.ap, apof(s1), apof(s2), op0, op1),
                   reads=regs(in0, s1, s2), writes=regs(out))

    def tt(out, in0, in1, op, eng="dve"):
        P.emit(eng, lambda e: e.tensor_tensor(out.ap, in0.ap, in1.ap, op),
               reads=regs(in0, in1), writes=regs(out))

    def stt(out, in0, scalar, in1, op0, op1, eng="dve"):
        P.emit(eng, lambda e: e.scalar_tensor_tensor(out.ap, in0.ap, apof(scalar), in1.ap, op0, op1),
               reads=regs(in0, scalar, in1), writes=regs(out))

    def cp(out, in_, eng="dve"):
        if eng == "act":
            P.emit(eng, lambda e: e.copy(out.ap, in_.ap), reads=regs(in_), writes=regs(out))
        else:
            P.emit(eng, lambda e: e.tensor_copy(out.ap, in_.ap), reads=regs(in_), writes=regs(out))

    def memset(out, val, eng="dve"):
        P.emit(eng, lambda e: e.memset(out.ap, val), writes=regs(out))

    def recip(out, in_):
        P.emit("dve", lambda e: e.reciprocal(out.ap, in_.ap), reads=regs(in_), writes=regs(out))

    def rsum(out, in_):
        P.emit("dve", lambda e: e.reduce_sum(out.ap, in_.ap, AX.X), reads=regs(in_), writes=regs(out))

    def bnstats(out, in_):
        P.emit("dve", lambda e: e.bn_stats(out.ap, in_.ap), reads=regs(in_), writes=regs(out))

    def bnaggr(out, in_):
        P.emit("dve", lambda e: e.bn_aggr(out.ap, in_.ap), reads=regs(in_), writes=regs(out))

    auto_slots = {}

    def dma(queue, out, in_, slot, final=False):
        sbside = out if isinstance(out, Opnd) else in_
        key = (queue, sbside.reg[1])
        if key not in auto_slots:
            auto_slots[key] = new_slot()
        slot = auto_slots[key]
        o = P.emit(queue, lambda e: e.dma_start(apof(out), apof(in_)),
                   reads=regs(in_), writes=regs(out), dma_slot=slot)
        if final:
            P.final_dma.append(o)
        return o

    def ln_stats(src, st6, mv, rs, nb):
        bnstats(st6[:, 0, :], Opnd(src.ap[:, 0:512], src.reg))
        bnstats(st6[:, 1, :], Opnd(src.ap[:, 512:1024], src.reg))
        bnaggr(mv[:, :], st6[:, :, :])
        act(rs, mv[:, 1:2], AF.Ln, bias=LN_EPS)
        act(rs, rs, AF.Exp, scale=-0.5)
        stt(nb, mv[:, 0:1], -1.0, rs, ALU.mult, ALU.mult)

    dma("sp", ident[:, :], ident_d, None)
    dma("sp", cT.flat(), cT_d, None)
    dma("sp", badaT[:, :], badaT_d, None)
    dma("sp", pscT[:, :], pscT_d, None)
    memset(ones[:, :], 1.0)
    cp(identb[:, :], ident[:, :])
    act(cactT[:, :, :], cT[:, :, :], AF.Silu)
    cp(CACTB[:, :, :], cactT[:, :, :])
    PIECE0 = [sb(OFF_HT, (8, 1024), BF16), sb(OFF_HT + 16 * K, (8, 1024), BF16)]
    pcount = 0
    bank_rr = [0]

    def next_bank():
        b = bank_rr[0]
        bank_rr[0] = (b + 1) % 8
        return b

    for k in (0, 1, 3, 4):
        pb = PIECE0[pcount % 2]
        dma("pool", pb.flat(), wada_d[k], None)
        pcount += 1
        for nt in range(8):
            bk = pbank(next_bank(), (512,))
            for c in range(8):
                mm(bk[:, 0:NSEQ], pb[:, c, nt * 128:(nt + 1) * 128], CACTB[:, c, :], c == 0, c == 7)
            i = k * 8 + nt
            if k in (1, 4):
                ts(modT[:, i, :], bk[:, 0:NSEQ], badaT[:, i:i + 1], 1.0, ALU.add, ALU.add)
            else:
                ts(modT[:, i, :], bk[:, 0:NSEQ], badaT[:, i:i + 1], None, ALU.add)

    def load_head_consts():
        dma("sp", biast.flat(), biast_d, None)
        dma("sp", fact.flat(), fact_d, None)
        dma("sp", distt.flat(), dist_d, None)
        dma("pool", amat.flat(), amat_d, None)
        dma("sp", subg[:, :], subg_d, None)
        dma("pool", wpool.flat(), wpool_d, None)
        ts(subg[:, :], subg[:, :], (1.0 - LAMBDA_INIT) * (128.0 ** 0.5), None, ALU.mult)

    def init_va():
        for b in range(2):
            memset(VA[b][:, :, 128:144], 1.0)
            memset(QTZ3[b][64:128, :, 0, :], 0.0)
            memset(QTZ3[b][0:64, :, 1, :], 0.0, eng="pool")

    def compute_lambda():
        dma("sp", lamv.flat(), lamv_d, None)
        tt(lamt[:, 0, :], lamv[:, 0, :], lamv[:, 1, :], ALU.mult)
        tt(lamt[:, 1, :], lamv[:, 2, :], lamv[:, 3, :], ALU.mult)
        rsum(lams[:, 0:1], lamt[:, 0, :])
        rsum(lams[:, 1:2], lamt[:, 1, :])
        act(lams[:, :], lams[:, :], AF.Exp)
        stt(neglam[:, :], lams[:, 1:2], -LAMBDA_INIT, lams[:, 0:1], ALU.add, ALU.subtract)

    ASTOP = 0

    def dbg_out(src):
        dma("sp", out_d[0:128, :], src, None, final=True)

    def phase_a0(s):
        r0 = s * T
        dma("pool", WU.flat(), wu_d, None)
        nbuf = 3
        for tt_ in range(min(2, NT)):
            dma("sp", XT[tt_ % nbuf][:, :], x_d[r0 + tt_ * 128: r0 + (tt_ + 1) * 128, :], None)
        for ti in range(NT):
            if ti + 2 < NT:
                dma("sp", XT[(ti + 2) % nbuf][:, :],
                    x_d[r0 + (ti + 2) * 128: r0 + (ti + 3) * 128, :], None)
            xt = XT[ti % nbuf]
            b = ti % 2
            ln_stats(xt[:, :], ST6[b], MV[b], RS[b][:, :], NB[b][:, :])
            if ASTOP == 1:
                cp(DBGT[:, 0:1], RS[b][:, :])
                cp(DBGT[:, 1:2], NB[b][:, :])
                cp(DBGT[:, 2:4], MV[b][:, :])
                dbg_out(DBGT[:, :])
                return
            act(XN[b][:, :], xt[:, :], AF.Identity, bias=NB[b][:, :], scale=RS[b][:, :])
            if ASTOP == 2:
                cp(DBGT[:, :], XN[b][:, :])
                dbg_out(DBGT[:, :])
                return
            for half in range(2):
                bk = pbank(next_bank(), (4, 128))
                for cc in range(4):
                    c = half * 4 + cc
                    if not None:
                        tr(bk[:, cc, :], XN[b][:, c * 128:(c + 1) * 128])
                for cc in range(4):
                    c = half * 4 + cc
                    if None:
                        continue
                    e = "act"
                    if None:
                        cp(hT[:, c, ti * 128:(ti + 1) * 128], bk[:, cc, :], eng=e)
                    elif e == "act":
                        act(hT[:, c, ti * 128:(ti + 1) * 128], bk[:, cc, :], AF.Identity,
                            bias=modT[:, 0 * 8 + c, s:s + 1], scale=modT[:, 1 * 8 + c, s:s + 1])
                    else:
                        ts(hT[:, c, ti * 128:(ti + 1) * 128], bk[:, cc, :],
                           modT[:, 1 * 8 + c, s:s + 1], modT[:, 0 * 8 + c, s:s + 1], ALU.mult, ALU.add)
            if ASTOP == 3:
                memset(DBGT[:, :], 0.0)
                if not None:
                    cp(DBGT[:, 0:128], hT[:, 0, 0:128])
                    cp(DBGT[:, 128:256], hT[:, 7, 0:128])
                dbg_out(DBGT[:, :])
                return
            bk = pbank(next_bank(), (512,))
            for c in range(8):
                mm(bk[:, :], hT[:, c, ti * 128:(ti + 1) * 128], WU[:, c, :], c == 0, c == 7)
            cp(UTOK[:, ti, :], bk[:, :], eng="act" if ti % 2 else "dve")
        for ti in range(NT):
            bk = pbank(next_bank(), (4, 128))
            for g in range(4):
                bands = []
                if ti > 0:
                    bands.append((ti - 1, 3))
                bands.append((ti, 0 if ti == 0 else (2 if ti == NT - 1 else 1)))
                if ti < NT - 1:
                    bands.append((ti + 1, 4))
                for bi, (tsrc, var) in enumerate(bands):
                    mm(bk[:, g, :], UTOK[:, tsrc, g * 128:(g + 1) * 128], amat[:, g, var, :],
                       bi == 0, bi == len(bands) - 1)
            cp(pooledT[:, :, ti * 128:(ti + 1) * 128], bk[:, :, :], eng="act" if ti % 2 else "dve")

    MISC = (6, 7)
    misc_rr = [0]

    def next_misc():
        b = MISC[misc_rr[0] % 2]
        misc_rr[0] += 1
        return b

    def head_proj(s, j, hb):
        w = WH[hb]
        g = j // 2
        for ch in range(NC5):
            tok = slice(ch * 512, (ch + 1) * 512)
            bk = pbank(next_misc(), (2, 256))
            for c in range(8):
                mm(bk.flat(), w[:, c, 0:128], hT[:, c, tok], c == 0, c == 7)
            cp(QTZ3[hb][0:64, 2 * ch:2 * ch + 2, 0, :], bk[0:64, :, :], eng="dve")
            cp(QTZ3[hb][64:128, 2 * ch:2 * ch + 2, 1, :], bk[64:128, :, :], eng="dve")
            bk = pbank(next_misc())
            for c in range(8):
                mm(bk[:, :], w[:, c, 128:256], hT[:, c, tok], c == 0, c == 7)
            cp(KT[hb][:, tok], bk[:, :], eng="act")
            bk = pbank(next_misc(), (4, 128))
            for t4 in range(4):
                ti = ch * 4 + t4
                for c in range(8):
                    mm(bk[:, t4, :], hT[:, c, ti * 128:(ti + 1) * 128], w[:, c, 256:384], c == 0, c == 7)
            cp(VA[hb][:, ch * 4:(ch + 1) * 4, 0:128], bk[:, :, :], eng="dve")
            bk = pbank(next_misc())
            for c in range(8):
                mm(bk[:, :], w[:, c, 384:512], hT[:, c, tok], c == 0, c == 7)
            act(SGA[hb][:, tok], bk[:, :], AF.Sigmoid)
            bk = pbank(next_misc())
            for c in range(8):
                mm(bk[:, :], w[:, c, 512:640], hT[:, c, tok], c == 0, c == 7)
            sg = SGP[0]
            act(sg[:, :], bk[:, :], AF.Sigmoid)
            bk = pbank(next_misc())
            mm(bk[:, :], wpool[:, g, (j % 2) * 128:(j % 2) * 128 + 128], pooledT[:, g, tok], True, True)
            stt(PP[hb][:, tok], bk[:, :], pscT[:, j:j + 1], sg[:, :], ALU.mult, ALU.mult)

    pt_rr = [0]
    sd_rr = [0]
    ep_rr = [0]
    aT_rr = [0]

    ATSTOP = 0

    pending = []

    def run_pending(force=False):
        items = pending[:]
        del pending[:]
        for item in items:
            item[0] -= 1
            if item[0] <= 0 or force:
                item[1]()
            else:
                pending.append(item)

    def flush_pending():
        guard = 0
        while pending and guard < 64:
            run_pending(force=True)
            guard += 1

    def attention(s, j, hb, mid_hook):
        slope = SLOPES[j]
        steps = [(c, kt) for c in range(NCH) for kt in range(NT)]
        nsteps_small = NT < 12

        def emit_qk(i):
            c, kt = steps[i]
            sbk = pbank(i % 3, (512,))
            ksl = slice(kt * 128, (kt + 1) * 128)
            mm(sbk[:, :], KT[hb][:, ksl], QTZ[hb][:, c * 512:(c + 1) * 512], True, True)

        def acc_views():
            base = 3 * 512
            A = [Buf("ps", psum_h[:, base: base + 320], base * 4, (2, 160), 4),
                 Buf("ps", psum_h[:, base + 672: base + 992], (base + 672) * 4, (2, 160), 4)]
            Bv = [Buf("ps", psum_h[:, base + 320: base + 704], (base + 320) * 4, (2, 192), 4),
                  Buf("ps", psum_h[:, base + 1024: base + 1344], (base + 1024) * 4, (2, 160), 4)]
            return A, Bv

        def epilogue(c, t, accs, e):
            hasL = c > 0
            hasR = c < NCH - 1
            Ar, Dr = accs[0][t][:, :, 0:129], accs[1][t][:, :, 0:129]
            FRf = fact[:, j, 2 * t + 1:2 * t + 2]
            ta, tb, tc = TMPA[0], TMPB[t], TMPC[e]
            tal = TAL[t]
            qt = slice((2 * c + t) * 128, (2 * c + t + 1) * 128)

            def s1a():
                if hasL and hasR:
                    stt(tb[:, :, :], Ar, FRf, tal[:, :, :], ALU.mult, ALU.add)

            def s1():
                if hasL and hasR:
                    tt(tc[:, :, :], Dr, tb[:, :, :], ALU.add)
                elif hasL:
                    tt(tc[:, :, :], Dr, tal[:, :, :], ALU.add)
                elif hasR:
                    act(ta[:, :, :], Dr, AF.Identity)
                    stt(tc[:, :, :], Ar, FRf, ta[:, :, :], ALU.mult, ALU.add)
                else:
                    act(tc[:, :, :], Dr, AF.Identity)

            def s2():
                recip(RDEN[e][:, :], tc[:, :, 128])
                ts(NL[e][:, :], RDEN[e][:, 1:2], neglam[:, :], None, ALU.mult)
                ts(O0[e][:, :], tc[:, 0, 0:128], RDEN[e][:, 0:1], None, ALU.mult)
                stt(OO[e][:, :], tc[:, 1, 0:128], NL[e][:, :], O0[e][:, :], ALU.mult, ALU.add)
                tt(SQ[e][:, :], OO[e][:, :], OO[e][:, :], ALU.mult)
                rsum(SS[e][:, :], SQ[e][:, :])

            def s3():
                act(RSTD[e][:, :], SS[e][:, :], AF.Ln, bias=128.0 * RMS_EPS)
                act(RSTD[e][:, :], RSTD[e][:, :], AF.Exp, scale=-0.5)

            def s4():
                stt(AOUT[e][:, :], OO[e][:, :], RSTD[e][:, :], subg[:, :], ALU.mult, ALU.mult)
                slot = aT_rr[0] % 8
                aT_rr[0] += 1
                bk = pbank(MISC[slot // 4], (4, 128))
                tr(bk[:, slot % 4, :], AOUT[e][:, :])

                def s5(bk=bk, slot=slot):
                    tt(GT1[e][:, :], bk[:, slot % 4, :], SGA[hb][:, qt], ALU.mult)
                    tt(mixedT[:, j, qt], GT1[e][:, :], PP[hb][:, qt], ALU.add)
                pending.append([2, s5])

            return s1a, s1, s2, s3, s4

        emit_qk(0)
        if len(steps) > 1:
            emit_qk(1)
        accs = acc_views()
        for i, (c, kt) in enumerate(steps):
            if kt == 0:
                if c == NCH // 2 and mid_hook is not None:
                    mid_hook()
            if i + 2 < len(steps):
                emit_qk(i + 2)
            sbk = pbank(i % 3, (2, 256))
            pt = PT[pt_rr[0] % 3]
            pt_rr[0] += 1
            if kt < 2 * c:
                cat = 0
                bidx = 2 * c - kt
                act(pt.flat(), sbk.flat(), AF.Exp, bias=biast[:, j, bidx:bidx + 1], scale=QK_SCALE)
            elif kt > 2 * c + 1:
                cat = 2
                bidx = 16 + (kt - 2 * c - 1)
                act(pt.flat(), sbk.flat(), AF.Exp, bias=biast[:, j, bidx:bidx + 1], scale=QK_SCALE)
            else:
                cat = 1
                ii = kt - 2 * c
                sd = SD[sd_rr[0] % 2]
                sd_rr[0] += 1
                stt(sd[:, :, :], distt[:, ii, :, :], -slope / QK_SCALE, sbk[:, :, :], ALU.mult, ALU.add)
                act(pt.flat(), sd.flat(), AF.Exp, scale=QK_SCALE)
            if cat == 0:
                first, last = kt == 0, kt == 2 * c - 1
            elif cat == 1:
                first, last = kt == 2 * c, kt == 2 * c + 1
            else:
                first, last = kt == 2 * c + 2, kt == NT - 1
            seen = set()
            for t in range(2):
                dst = accs[1][t] if cat == 1 else accs[0][t]
                for m in range(2):
                    if cat == 1:
                        bank = 3 if (t == 0 and m == 0) else (4 if t == 0 else 5)
                    else:
                        bank = 3 if t == 0 else 4
                    st = first and (bank not in seen)
                    seen.add(bank)
                    mm(dst[:, m, 0:132], pt[:, m, t * 128:(t + 1) * 128], VA[hb][:, kt, 0:132], st, last)
            if cat == 0 and last:
                for t in range(2):
                    act(TAL[t][:, :, :], accs[0][t][:, :, 0:129], AF.Identity, scale=fact[:, j, 2 * t:2 * t + 1])
            run_pending()
            if kt == NT - 1:
                if NT < 12:
                    flush_pending()
                st = []
                for t in range(2):
                    e = ep_rr[0] % 2
                    ep_rr[0] += 1
                    st.append(epilogue(c, t, accs, e))
                st[0][0]()
                st[1][0]()
                st[0][1]()
                st[1][1]()
                for t in range(2):
                    pending.append([1 + t, st[t][2]])
                    pending.append([3 + t, st[t][3]])
                    pending.append([5 + t, st[t][4]])
        if nsteps_small:
            flush_pending()
        return False

    def bcast_mod_tile(s, k, piece, gtile, bidx):
        dma("pool", piece.flat(), wada_d[k], None)
        dma("sp", BADAG[:, :], badag_d[:, bidx * 1024:(bidx + 1) * 1024], None)
        for c in range(8):
            act(CBT[:, c, :], ones[:, :], AF.Identity, scale=cactT[:, c, s:s + 1])
        for half in range(2):
            bk = pbank(next_bank())
            for c in range(8):
                mm(bk[:, :], CBT[:, c, :], piece[:, c, half * 512:(half + 1) * 512], c == 0, c == 7)
            tt(gtile[:, half * 512:(half + 1) * 512], bk[:, :], BADAG[:, half * 512:(half + 1) * 512], ALU.add)

    def phase_c(s):
        r0 = s * T
        dma("pool", WOUT.flat(), wout_d, None)
        dma("sp", LN1.flat(), ln_d[:, 0:2048], None)
        bcast_mod_tile(s, 2, PIECE_C, G1T, 0)
        dma("sp", CX[0][:, :], x_d[r0: r0 + 128, :], None)
        for ti in range(NT):
            b = ti % 2
            if ti + 1 < NT:
                dma("sp", CX[1 - b][:, :], x_d[r0 + (ti + 1) * 128: r0 + (ti + 2) * 128, :], None)
            r = CR[b]
            for half in range(2):
                bk = pbank(next_bank())
                hs = slice(half * 512, (half + 1) * 512)
                for c in range(8):
                    mm(bk[:, :], mixedT[:, c, ti * 128:(ti + 1) * 128], WOUT[:, c, hs], c == 0, c == 7)
                tt(r[:, hs], bk[:, :], G1T[:, hs], ALU.mult)
            stt(r[:, :], CX[b][:, :], ALPHA, r[:, :], ALU.mult, ALU.add)
            ln_stats(r[:, :], CST6[b], CMV[b], CRS[b][:, :], CNB[b][:, :])
            act(r[:, :], r[:, :], AF.Identity, bias=CNB[b][:, :], scale=CRS[b][:, :])
            tt(r[:, :], r[:, :], LN1[:, 0, :], ALU.mult, eng="pool")
            tt(x1[:, ti, :], r[:, :], LN1[:, 1, :], ALU.add, eng="pool")
            ln_stats(x1[:, ti, :], CST6b[b], CMVb[b], RS2[:, ti:ti + 1], NB2[:, ti:ti + 1])

    def phase_ffn(s):
        r0 = s * T
        dma("pool", WFO.flat(), wfo_d, None)
        dma("sp", LN2.flat(), ln_d[:, 2048:4096], None)
        bcast_mod_tile(s, 5, PIECE_F, G2T, 1)
        nw = 3
        wcount = [0]

        def issue_wfi(ft):
            k_ = wcount[0] % nw
            wcount[0] += 1
            dma("pool", WFI[k_].flat(), wfi_d[ft], None)
            return k_

        for ch in range(NC5):
            h2 = H2T[0]
            pend = [issue_wfi(0), issue_wfi(1)]
            for t4 in range(4):
                ti = ch * 4 + t4
                b = ti % 2
                act(FXN[b][:, :], x1[:, ti, :], AF.Identity, bias=NB2[:, ti:ti + 1], scale=RS2[:, ti:ti + 1])
                for half in range(2):
                    bk = pbank(next_bank(), (4, 128))
                    for cc in range(4):
                        c = half * 4 + cc
                        tr(bk[:, cc, :], FXN[b][:, c * 128:(c + 1) * 128])
                    for cc in range(4):
                        c = half * 4 + cc
                        if True:
                            act(h2[:, c, t4 * 128:(t4 + 1) * 128], bk[:, cc, :], AF.Identity,
                                bias=modT[:, 3 * 8 + c, s:s + 1], scale=modT[:, 4 * 8 + c, s:s + 1])
                        else:
                            ts(h2[:, c, t4 * 128:(t4 + 1) * 128], bk[:, cc, :],
                               modT[:, 4 * 8 + c, s:s + 1], modT[:, 3 * 8 + c, s:s + 1], ALU.mult, ALU.add)
            for ft in range(NFT):
                wk = WFI[pend.pop(0)]
                if ft + 2 < NFT:
                    pend.append(issue_wfi(ft + 2))
                bg = pbank(next_bank())
                for c in range(8):
                    mm(bg[:, :], wk[:, c, 0:128], h2[:, c, :], c == 0, c == 7)
                bu = pbank(next_bank())
                for c in range(8):
                    mm(bu[:, :], wk[:, c, 128:256], h2[:, c, :], c == 0, c == 7)
                sg = SG[ft % 2]
                act(sg[:, :], bg[:, :], AF.Silu)
                tt(ACTT[:, ft, :], bu[:, :], sg[:, :], ALU.mult)
            for t4 in range(4):
                ti = ch * 4 + t4
                b = ti % 2
                r = FR_[b]
                for half in range(2):
                    bk = pbank(next_bank())
                    hs = slice(half * 512, (half + 1) * 512)
                    for ft in range(NFT):
                        mm(bk[:, :], ACTT[:, ft, t4 * 128:(t4 + 1) * 128], WFO[:, ft, hs], ft == 0, ft == NFT - 1)
                    tt(r[:, hs], bk[:, :], G2T[:, hs], ALU.mult)
                stt(r[:, :], x1[:, ti, :], ALPHA, r[:, :], ALU.mult, ALU.add)
                ln_stats(r[:, :], FST6[b], FMV[b], FRS[b][:, :], FNB[b][:, :])
                act(r[:, :], r[:, :], AF.Identity, bias=FNB[b][:, :], scale=FRS[b][:, :])
                tt(r[:, :], r[:, :], LN2[:, 0, :], ALU.mult, eng="pool")
                tt(r[:, :], r[:, :], LN2[:, 1, :], ALU.add, eng="pool")
                dma("sp", out_d[r0 + ti * 128: r0 + (ti + 1) * 128, :], r[:, :], None, final=True)

    compute_lambda_done = [False]
    KSTOP = 0

    def dbg_out(src):
        dma("sp", out_d[0:128, :], src, None, final=True)

    MICRO = 0
    for s in range(NSEQ if not MICRO else 0):
        if KSTOP == 1:
            cp(DBGT[:, 0:48 * NSEQ], Opnd(modT.ap2d, modT.flat().reg))
            dbg_out(DBGT[:, :])
            break
        load_head_consts()
        if not compute_lambda_done[0]:
            compute_lambda()
            compute_lambda_done[0] = True
        if KSTOP == 2:
            cp(DBGT[:, 0:1], neglam[:, :])
            cp(DBGT[:, 128:256], subg[:, :])
            dbg_out(DBGT[:, :])
            break
        phase_a0(s)
        if ASTOP:
            break
        init_va()
        if KSTOP == 3:
            cp(DBGT[:, 0:512], pooledT[:, 0, 0:512])
            cp(DBGT[:, 512:1024], hT[:, 0, 0:512])
            dbg_out(DBGT[:, :])
            break
        dma("pool", WH[0].flat(), wh_d[0], None)
        head_proj(s, 0, 0)
        if KSTOP == 4:
            cp(DBGT[:, 0:512], QTZ[0][:, 0:512])
            cp(DBGT[:, 512:1024], PP[0][:, 0:512])
            dbg_out(DBGT[:, :])
            break
        for j in range(NH):
            hb = j % 2
            hook = None
            if j + 1 < NH:
                dma("pool", WH[1 - hb].flat(), wh_d[j + 1], None)
                hook = (lambda jj=j + 1, bb=1 - hb: head_proj(s, jj, bb))
            if attention(s, j, hb, hook):
                break
            if hook is not None and NCH // 2 == 0:
                hook()
            if KSTOP == 5:
                break
        if ATSTOP:
            break
        if KSTOP in (5, 6):
            cp(DBGT[:, 0:512], mixedT[:, 0, 0:512])
            cp(DBGT[:, 512:1024], mixedT[:, 7, 0:512])
            dbg_out(DBGT[:, :])
            break
        flush_pending()
        phase_c(s)
        if KSTOP == 7:
            dbg_out(x1[:, 0, :])
            break
        phase_ffn(s)

    sem_cms = {e: nc.semaphore("s_" + e) for e in ("pe", "act", "dve", "pool")}
    sems = {e: cm.__enter__() for e, cm in sem_cms.items()}
    for e in ("pe", "act", "dve", "pool"):
        P.replay(e, None, sems)
    with nc.Block() as block:
        @block.tensor
        def _(eng):
            P.run_engine("pe", eng, sems)

        @block.scalar
        def _(eng):
            P.run_engine("act", eng, sems)

        @block.vector
        def _(eng):
            P.run_engine("dve", eng, sems)

        @block.gpsimd
        def _(eng):
            P.run_engine("pool", eng, sems)

        @block.sync
        def _(eng):
            P.run_engine("sp", eng, sems)
    return nc, P


def _chunk_rows(w):
    k, n = w.shape
    return np.ascontiguousarray(w.reshape(k // 128, 128, n).transpose(1, 0, 2))


def prep_shared(inp):
    f = lambda a: np.ascontiguousarray(np.asarray(a, dtype=np.float32))
    w_ada = f(inp["w_ada"])[0]
    b_ada = f(inp["b_ada"])[0]
    w_in = f(inp["w_in"])[0]
    sh = {}
    sh["wada"] = np.ascontiguousarray(
        np.stack([_chunk_rows(w_ada[:, k * 1024:(k + 1) * 1024]).reshape(128, 8 * 1024) for k in range(6)]))
    sh["badaT"] = np.ascontiguousarray(b_ada.reshape(48, 128).T)
    sh["badag"] = np.ascontiguousarray(
        np.broadcast_to(np.concatenate([b_ada[2048:3072], b_ada[5120:6144]])[None, :], (128, 2048)))
    sh["wu"] = _chunk_rows(w_in[:, 3072:3584]).reshape(128, 8 * 512)
    wh = []
    for j in range(NH):
        cols = np.concatenate([
            w_in[:, j * 128:(j + 1) * 128],
            w_in[:, 1024 + j * 128:1024 + (j + 1) * 128],
            w_in[:, 2048 + j * 128:2048 + (j + 1) * 128],
            w_in[:, 3584 + j * 128:3584 + (j + 1) * 128],
            w_in[:, 4608 + j * 128:4608 + (j + 1) * 128]], axis=1)
        wh.append(_chunk_rows(cols).reshape(128, 8 * 640))
    sh["wh"] = np.ascontiguousarray(np.stack(wh))
    sh["wpool"] = np.ascontiguousarray(f(inp["w_pool"])[0].transpose(1, 0, 2)).reshape(128, 4 * 256)
    sh["wout"] = _chunk_rows(f(inp["w_out"])[0]).reshape(128, 8 * 1024)
    wfi_full = f(inp["w_ffn_in"])[0]
    wfi = []
    for ft in range(NFT):
        cols = np.concatenate([wfi_full[:, ft * 128:(ft + 1) * 128],
                               wfi_full[:, DFF + ft * 128:DFF + (ft + 1) * 128]], axis=1)
        wfi.append(_chunk_rows(cols).reshape(128, 8 * 256))
    sh["wfi"] = np.ascontiguousarray(np.stack(wfi))
    sh["wfo"] = _chunk_rows(f(inp["w_ffn_out"])[0]).reshape(128, NFT * 1024)
    lam = np.stack([f(inp["lambda_q1"])[0], f(inp["lambda_k1"])[0],
                    f(inp["lambda_q2"])[0], f(inp["lambda_k2"])[0]]).reshape(1, 256)
    sh["lamv"] = np.ascontiguousarray(np.broadcast_to(lam, (128, 256)))
    sh["subg"] = np.ascontiguousarray(np.broadcast_to(f(inp["sub_g"])[0][None, :], (128, 128)))
    sh["pscT"] = np.ascontiguousarray(f(inp["pool_scale"])[0].reshape(8, 128).T)
    lnp = np.concatenate([f(inp["ln1_g"])[0], f(inp["ln1_b"])[0], f(inp["ln2_g"])[0], f(inp["ln2_b"])[0]])
    sh["lnp"] = np.ascontiguousarray(np.broadcast_to(lnp[None, :], (128, 4096)))
    ident, bias, fac, dist, amat = _make_tables()
    sh["ident"] = ident
    sh["biast"] = bias.reshape(128, NH * 32)
    sh["fact"] = fac.reshape(128, NH * 4)
    sh["dist"] = dist.reshape(128, 1024)
    sh["amat"] = amat.reshape(128, 4 * 5 * 128)
    return sh


def prep_core(x, c, core, T, NSEQ):
    xs = np.ascontiguousarray(x[core * NSEQ:(core + 1) * NSEQ].reshape(NSEQ * T, D), dtype=np.float32)
    cs = np.asarray(c[core * NSEQ:(core + 1) * NSEQ], dtype=np.float32)
    cT = np.ascontiguousarray(cs.T.reshape(8, 128, NSEQ).transpose(1, 0, 2)).reshape(128, 8 * NSEQ)
    return {"x": xs, "cT": cT}


def kernel(**inputs):
    x = np.asarray(inputs["x"], dtype=np.float32)
    c = np.asarray(inputs["c"], dtype=np.float32)
    B, T, _ = x.shape
    NSEQ = B // N_CORES
    nc, _ = build(T, NSEQ)
    sh = prep_shared(inputs)
    in_maps = []
    for core in range(N_CORES):
        m = dict(sh)
        m.update(prep_core(x, c, core, T, NSEQ))
        in_maps.append(m)
    res = run_bass_kernel_spmd(nc, in_maps, core_ids=list(range(N_CORES)))
    outs = [np.asarray(r["out"], dtype=np.float32).reshape(NSEQ, T, D) for r in res.results]
    return np.concatenate(outs, axis=0)
```
